# Optimizing a Trainium2 kernel written in Bass

```python
import jax
import jax.numpy as jnp
from jax import lax
import numpy as np

D_MODEL = 1024
BATCH = 8
SEQ = 2048
DEPTH = 4
DEC_BATCH = 32
DEC_SEQ = 4
PAST_LEN = 16384
PAGE_SIZE = 128

N_MIXERS = 3
N_A = (DEPTH + 2) // 3
N_B = (DEPTH + 1) // 3
N_C = DEPTH // 3

CHUNK_A = 128
SGU_DIM = D_MODEL
A_GROUPS = 8
A_GDIM = SGU_DIM // A_GROUPS
B_HEADS = 8
B_DK = 128
B_DV = D_MODEL // B_HEADS
B_CHUNK = 64
C_HEADS = 8
C_NOPE = 128
C_ROPE = 64
C_V = 128
C_QLORA = 512
C_KVLORA = 256
ROPE_THETA = 10000.0
Q_BLOCK = 128
D_FF = 4 * D_MODEL
ALPHA = (2.0 * DEPTH) ** 0.25
BETA = (8.0 * DEPTH) ** -0.25
EPS = 1e-6
F32 = jnp.float32

kernel_name = 'hybrid_gmlp_hgrn2_mla_deepnorm_adaln_step'


def layer_norm(x, g, b):
    xf = x.astype(F32)
    mu = jnp.mean(xf, -1, keepdims=True)
    var = jnp.mean(jnp.square(xf - mu), -1, keepdims=True)
    return ((xf - mu) * lax.rsqrt(var + EPS) * g.astype(F32) + b.astype(F32)).astype(x.dtype)


def rms_norm(x, g=None):
    xf = x.astype(F32)
    y = xf * lax.rsqrt(jnp.mean(xf * xf, -1, keepdims=True) + EPS)
    if g is not None:
        y = y * g.astype(F32)
    return y.astype(x.dtype)


def rope(x, pos):
    half = x.shape[-1] // 2
    inv = ROPE_THETA ** (-jnp.arange(half, dtype=F32) / half)
    ang = pos.astype(F32)[:, None] * inv
    ang = ang.reshape((1, ang.shape[0]) + (1,) * (x.ndim - 3) + (half,))
    cos, sin = jnp.cos(ang), jnp.sin(ang)
    x1 = x[..., :half].astype(F32)
    x2 = x[..., half:].astype(F32)
    return jnp.concatenate([x1 * cos - x2 * sin, x2 * cos + x1 * sin], -1).astype(x.dtype)


def chunk_mlp_mixer(h, w_in, ln_g, ln_b, w_s, b_s, w_out):
    B, L, _ = h.shape
    z = jax.nn.gelu(h @ w_in)
    u, v = jnp.split(z, 2, axis=-1)
    v = layer_norm(v, ln_g, ln_b)
    C = min(CHUNK_A, L)
    causal = jnp.tril(jnp.ones((C, C), dtype=bool))
    ws = jnp.where(causal[None], w_s[:, :C, :C], 0)
    vc = v.reshape(B, L // C, C, A_GROUPS, A_GDIM)
    mixed = jnp.einsum('gts,bnsgd->bntgd', ws, vc) + b_s[:, :C].T[None, None, :, :, None]
    out = u * mixed.reshape(B, L, SGU_DIM)
    return out @ w_out, v


def gated_linear_recurrence(q, k, v, log_f, state0):
    B, L, H, _ = q.shape
    C = min(B_CHUNK, L)
    n = L // C

    def to_chunks(t):
        return t.astype(F32).reshape(B, n, C, H, t.shape[-1]).transpose(1, 0, 3, 2, 4)

    qc, kc, vc, gc = to_chunks(q), to_chunks(k), to_chunks(v), to_chunks(log_f)
    causal = jnp.tril(jnp.ones((C, C), dtype=bool))

    def step(S, inp):
        qi, ki, vi, gi = inp
        b = jnp.cumsum(gi, axis=2)
        o_inter = jnp.einsum('bhtd,bhdv->bhtv', qi * jnp.exp(b), S)
        diff = b[:, :, :, None, :] - b[:, :, None, :, :]
        decay = jnp.exp(jnp.where(causal[:, :, None], diff, -jnp.inf))
        A = jnp.einsum('bhtd,bhtsd,bhsd->bhts', qi, decay, ki)
        o_intra = jnp.einsum('bhts,bhsv->bhtv', A, vi)
        b_last = b[:, :, -1:, :]
        S_new = jnp.exp(b_last[:, :, 0, :, None]) * S + jnp.einsum(
            'bhsd,bhsv->bhdv', ki * jnp.exp(b_last - b), vi)
        return S_new, o_inter + o_intra

    S, o = lax.scan(step, state0.astype(F32), (qc, kc, vc, gc))
    o = o.transpose(1, 0, 3, 2, 4).reshape(B, L, H, -1)
    return o.astype(q.dtype), S.astype(state0.dtype)


def hgrn2_mixer(h, w_in, lb, state0, w_out):
    B, L, _ = h.shape
    q, fz, i_in, g = jnp.split(h @ w_in, 4, axis=-1)
    q = jax.nn.silu(q).reshape(B, L, B_HEADS, B_DK)
    fz = fz.astype(F32).reshape(B, L, B_HEADS, B_DK)
    lbh = lb.astype(F32).reshape(B_HEADS, B_DK)
    log_f = jnp.logaddexp(jnp.log(lbh), jnp.log1p(-lbh) + jax.nn.log_sigmoid(fz))
    k = (1.0 - lbh) * jax.nn.sigmoid(-fz)
    v = i_in.reshape(B, L, B_HEADS, B_DV)
    if state0 is None:
        state0 = jnp.zeros((B, B_HEADS, B_DK, B_DV), dtype=h.dtype)
    o, S = gated_linear_recurrence(q, k, v, log_f, state0)
    o = rms_norm(o).reshape(B, L, B_HEADS * B_DV) * jax.nn.silu(g)
    return o @ w_out, S


def latent_attention(q_lat, q_rope, k_lat, k_rope, q_pos, k_pos):
    B, L, H, _ = q_lat.shape
    blk = min(Q_BLOCK, L)
    n = L // blk
    scale = (C_NOPE + C_ROPE) ** -0.5

    def one_block(args):
        ql, qr, qp = args
        s = (jnp.einsum('bqhc,bkc->bhqk', ql, k_lat, preferred_element_type=F32)
             + jnp.einsum('bqhr,bkr->bhqk', qr, k_rope, preferred_element_type=F32)) * scale
        s = jnp.where(k_pos[None, None, None, :] <= qp[None, None, :, None], s, -jnp.inf)
        p = jax.nn.softmax(s, axis=-1)
        return jnp.einsum('bhqk,bkc->bqhc', p.astype(k_lat.dtype), k_lat)

    qb = q_lat.reshape(B, n, blk, H, -1).transpose(1, 0, 2, 3, 4)
    rb = q_rope.reshape(B, n, blk, H, -1).transpose(1, 0, 2, 3, 4)
    out = lax.map(one_block, (qb, rb, q_pos.reshape(n, blk)))
    return out.transpose(1, 0, 2, 3, 4).reshape(B, L, H, -1)


def mla_mixer(h, q_pos, past, w_in, g_q, g_kv, w_uq, w_uk, w_uv, w_out):
    B, L, _ = h.shape
    a = h @ w_in
    cq, ckv, kr = jnp.split(a, [C_QLORA, C_QLORA + C_KVLORA], axis=-1)
    cq = rms_norm(cq, g_q)
    ckv = rms_norm(ckv, g_kv)
    kr = rope(kr, q_pos)
    q = jnp.einsum('blc,chd->blhd', cq, w_uq)
    q_nope = q[..., :C_NOPE]
    q_rope = rope(q[..., C_NOPE:], q_pos)
    q_lat = jnp.einsum('blhd,chd->blhc', q_nope, w_uk)
    if past is None:
        k_lat, k_rope, k_pos = ckv, kr, q_pos
    else:
        p_lat, p_rope = past
        k_lat = jnp.concatenate([p_lat.astype(ckv.dtype), ckv], axis=1)
        k_rope = jnp.concatenate([p_rope.astype(kr.dtype), kr], axis=1)
        k_pos = jnp.concatenate([jnp.arange(p_lat.shape[1], dtype=jnp.int32), q_pos])
    o_lat = latent_attention(q_lat, q_rope, k_lat, k_rope, q_pos, k_pos)
    o = jnp.einsum('blhc,chv->blhv', o_lat, w_uv).reshape(B, L, C_HEADS * C_V)
    return o @ w_out, ckv, kr


def squared_relu_mlp(h, w1, w2):
    return jnp.square(jax.nn.relu(h @ w1)) @ w2


def gather_pages(pool, page_table):
    g = pool[page_table]
    return g.reshape(g.shape[0], g.shape[1] * g.shape[2], g.shape[3])


def run_trunk(x, c, q_pos, hgrn_state0, mla_cache, prm):
    lb_all = jnp.cumsum(jax.nn.softmax(prm['b_lb'].astype(F32), axis=0), axis=0)
    lb_all = lb_all - lb_all[:1]
    chunk_v, hgrn_states, lat_rows, rope_rows = [], [], [], []
    sc = jax.nn.silu(c)
    for i in range(DEPTH):
        kind, j = i % N_MIXERS, i // N_MIXERS
        mod = sc @ prm['w_ada'][i] + prm['b_ada'][i]
        sh1, sc1, g1, sh2, sc2, g2 = jnp.split(mod[:, None, :], 6, axis=-1)
        h = x * (1 + sc1) + sh1
        if kind == 0:
            out, v_rows = chunk_mlp_mixer(h, prm['a_w_in'][j], prm['a_ln_g'][j], prm['a_ln_b'][j],
                                          prm['a_w_s'][j], prm['a_b_s'][j], prm['a_w_out'][j])
            chunk_v.append(v_rows)
        elif kind == 1:
            s0 = None if hgrn_state0 is None else hgrn_state0[j]
            out, S = hgrn2_mixer(h, prm['b_w_in'][j], lb_all[i], s0, prm['b_w_out'][j])
            hgrn_states.append(S)
        else:
            if mla_cache is None:
                past = None
            else:
                pool_lat, pool_rope, pt = mla_cache
                past = (gather_pages(pool_lat[j], pt), gather_pages(pool_rope[j], pt))
            out, lat, kr = mla_mixer(h, q_pos, past, prm['c_w_in'][j], prm['c_g_q'][j],
                                     prm['c_g_kv'][j], prm['c_w_uq'][j], prm['c_w_uk'][j],
                                     prm['c_w_uv'][j], prm['c_w_out'][j])
            lat_rows.append(lat)
            rope_rows.append(kr)
        x = layer_norm(ALPHA * x + g1 * out, prm['ln1_g'][i], prm['ln1_b'][i])
        h = x * (1 + sc2) + sh2
        x = layer_norm(ALPHA * x + g2 * squared_relu_mlp(h, prm['ffn_w1'][i], prm['ffn_w2'][i]),
                       prm['ln2_g'][i], prm['ln2_b'][i])
    return x, jnp.stack(chunk_v), jnp.stack(hgrn_states), jnp.stack(lat_rows), jnp.stack(rope_rows)


def setup_inputs(seed: int = 0) -> dict:
    key = jax.random.key(seed)
    ks = iter(jax.random.split(key, 40))

    def nrm(shape, scale):
        return jax.random.normal(next(ks), shape, F32) * scale

    n_pages = PAST_LEN // PAGE_SIZE
    n_pool = (DEC_BATCH * n_pages * 5) // 4
    page_table = jax.random.permutation(next(ks), n_pool)[: DEC_BATCH * n_pages]
    page_table = page_table.reshape(DEC_BATCH, n_pages).astype(jnp.int32)
    d = D_MODEL
    return {
        'x_prompt': nrm((BATCH, SEQ, d), 1.0),
        'x_sample': nrm((DEC_BATCH, DEC_SEQ, d), 1.0),
        'cache_kv_latent': nrm((N_C, n_pool, PAGE_SIZE, C_KVLORA), 1.0),
        'cache_k_rope': nrm((N_C, n_pool, PAGE_SIZE, C_ROPE), 1.0),
        'state_hgrn': nrm((N_B, DEC_BATCH, B_HEADS, B_DK, B_DV), 0.5),
        'page_table': page_table,
        'c_prompt': nrm((BATCH, d), 1.0),
        'c_sample': nrm((DEC_BATCH, d), 1.0),
        'w_ada': nrm((DEPTH, d, 6 * d), 0.5 * d ** -0.5),
        'b_ada': nrm((DEPTH, 6 * d), 0.01),
        'ln1_g': 1.0 + nrm((DEPTH, d), 0.05),
        'ln1_b': nrm((DEPTH, d), 0.01),
        'ln2_g': 1.0 + nrm((DEPTH, d), 0.05),
        'ln2_b': nrm((DEPTH, d), 0.01),
        'ffn_w1': nrm((DEPTH, d, D_FF), d ** -0.5),
        'ffn_w2': nrm((DEPTH, D_FF, d), BETA * D_FF ** -0.5),
        'a_w_in': nrm((N_A, d, 2 * SGU_DIM), d ** -0.5),
        'a_ln_g': 1.0 + nrm((N_A, SGU_DIM), 0.05),
        'a_ln_b': nrm((N_A, SGU_DIM), 0.01),
        'a_w_s': nrm((N_A, A_GROUPS, CHUNK_A, CHUNK_A), CHUNK_A ** -0.5),
        'a_b_s': 1.0 + nrm((N_A, A_GROUPS, CHUNK_A), 0.1),
        'a_w_out': nrm((N_A, SGU_DIM, d), BETA * SGU_DIM ** -0.5),
        'b_w_in': nrm((N_B, d, 4 * d), d ** -0.5),
        'b_lb': 1.0 + nrm((DEPTH, B_HEADS * B_DK), 0.1),
        'b_w_out': nrm((N_B, B_HEADS * B_DV, d), BETA * (B_HEADS * B_DV) ** -0.5),
        'c_w_in': nrm((N_C, d, C_QLORA + C_KVLORA + C_ROPE), d ** -0.5),
        'c_g_q': 1.0 + nrm((N_C, C_QLORA), 0.05),
        'c_g_kv': 1.0 + nrm((N_C, C_KVLORA), 0.05),
        'c_w_uq': nrm((N_C, C_QLORA, C_HEADS, C_NOPE + C_ROPE), C_QLORA ** -0.5),
        'c_w_uk': nrm((N_C, C_KVLORA, C_HEADS, C_NOPE), C_KVLORA ** -0.5),
        'c_w_uv': nrm((N_C, C_KVLORA, C_HEADS, C_V), C_KVLORA ** -0.5),
        'c_w_out': nrm((N_C, C_HEADS * C_V, d), BETA * (C_HEADS * C_V) ** -0.5),
    }


def reference(x_prompt, x_sample, cache_kv_latent, cache_k_rope, state_hgrn, page_table,
              c_prompt, c_sample, w_ada, b_ada, ln1_g, ln1_b, ln2_g, ln2_b, ffn_w1, ffn_w2,
              a_w_in, a_ln_g, a_ln_b, a_w_s, a_b_s, a_w_out, b_w_in, b_lb, b_w_out,
              c_w_in, c_g_q, c_g_kv, c_w_uq, c_w_uk, c_w_uv, c_w_out):
    prm = dict(w_ada=w_ada, b_ada=b_ada, ln1_g=ln1_g, ln1_b=ln1_b, ln2_g=ln2_g, ln2_b=ln2_b,
               ffn_w1=ffn_w1, ffn_w2=ffn_w2, a_w_in=a_w_in, a_ln_g=a_ln_g, a_ln_b=a_ln_b,
               a_w_s=a_w_s, a_b_s=a_b_s, a_w_out=a_w_out, b_w_in=b_w_in, b_lb=b_lb,
               b_w_out=b_w_out, c_w_in=c_w_in, c_g_q=c_g_q, c_g_kv=c_g_kv, c_w_uq=c_w_uq,
               c_w_uk=c_w_uk, c_w_uv=c_w_uv, c_w_out=c_w_out)
    past_len = page_table.shape[1] * cache_kv_latent.shape[2]
    pos_prompt = jnp.arange(x_prompt.shape[1], dtype=jnp.int32)
    pos_sample = past_len + jnp.arange(x_sample.shape[1], dtype=jnp.int32)
    y_prompt, _, hs_p, lat_p, rope_p = run_trunk(x_prompt, c_prompt, pos_prompt, None, None, prm)
    y_sample, v_s, hs_s, lat_s, rope_s = run_trunk(
        x_sample, c_sample, pos_sample, state_hgrn, (cache_kv_latent, cache_k_rope, page_table), prm)
    return (y_prompt, y_sample, hs_p, hs_s, lat_p, rope_p, lat_s, rope_s, v_s)
```

```python
import numpy as np
from contextlib import ExitStack
import concourse.bass as bass
import concourse.mybir as mybir

F32 = mybir.dt.float32
BF16 = mybir.dt.bfloat16
I32 = mybir.dt.int32
U32 = mybir.dt.uint32
AF = mybir.ActivationFunctionType
ALU = mybir.AluOpType
AX = mybir.AxisListType

PE, ACT, DVE, POOL, SP = "tensor", "scalar", "vector", "gpsimd", "sync"
COMPUTE = (PE, ACT, DVE, POOL)


class Buf:
    __slots__ = ("ap", "name", "lw", "rd", "dsem", "dcnt")

    def __init__(self, ap, name):
        self.ap = ap
        self.name = name
        self.lw = []
        self.rd = []
        self.dsem = None
        self.dcnt = 0

    def __getitem__(self, key):
        return V(self.ap[key], self)

    @property
    def v(self):
        return V(self.ap, self)


class V:
    __slots__ = ("ap", "buf")

    def __init__(self, ap, buf):
        self.ap = ap
        self.buf = buf

    def __getitem__(self, key):
        return V(self.ap[key], self.buf)

    def bitcast(self, dt):
        return V(self.ap.bitcast(dt), self.buf)

    def rearrange(self, s, **kw):
        return V(self.ap.rearrange(s, **kw), self.buf)

    def bc(self, shape):
        return V(self.ap.broadcast_to(shape), self.buf)


class Op:
    __slots__ = ("eng", "fn", "deps", "is_dma", "sem", "val", "milestone", "id")


class KB:
    def __init__(self, nc):
        self.nc = nc
        self.es = ExitStack()
        self.ops = []
        self.nsem = 0
        self.sems = []
        self.dma_sems = []
        self.n_alloc = 0

    def dram_in(self, name, shape, dtype=F32):
        return self.nc.dram_tensor(name, list(shape), dtype, kind="ExternalInput").ap()

    def dram_out(self, name, shape, dtype=F32):
        return self.nc.dram_tensor(name, list(shape), dtype, kind="ExternalOutput").ap()

    def sb(self, name, shape, dtype=F32):
        t = self.es.enter_context(self.nc.sbuf_tensor(name, list(shape), dtype))
        return Buf(t[:] if len(shape) == 1 else t[tuple(slice(None) for _ in shape)], name)

    def ps(self, name, shape, dtype=F32):
        t = self.es.enter_context(self.nc.psum_tensor(name, list(shape), dtype))
        return Buf(t[tuple(slice(None) for _ in shape)], name)

    def new_sem(self, name):
        s = self.es.enter_context(self.nc.semaphore(name))
        return s

    def _deps(self, eng, reads, writes, is_dma, dsem_buf):
        deps = set()
        for b in reads:
            deps.update(b.lw)
        for b in writes:
            deps.update(b.lw)
            deps.update(b.rd)
        return deps

    def _dma_deps(self, rb, wb, dsem):
        deps = set()
        for b in rb:
            deps.update(b.lw)
        for b in wb:
            deps.update(b.rd)
            for d in b.lw:
                if not (self.ops[d].is_dma and self.ops[d].sem == dsem):
                    deps.add(d)
        return deps

    def op(self, eng, fn, reads=(), writes=()):
        rb = []
        for r in reads:
            if r is None:
                continue
            b = r.buf if isinstance(r, V) else r
            if b not in rb:
                rb.append(b)
        wb = []
        for w in writes:
            if w is None:
                continue
            b = w.buf if isinstance(w, V) else w
            if b not in wb:
                wb.append(b)
        o = Op()
        o.eng = eng
        o.fn = fn
        o.is_dma = False
        o.deps = self._deps(eng, rb, wb, False, None)
        o.sem = None
        o.val = None
        o.milestone = False
        o.id = len(self.ops)
        self.ops.append(o)
        for b in wb:
            b.lw = [o.id]
            b.rd = []
        for b in rb:
            if b not in wb:
                b.rd = [d for d in b.rd if self.ops[d].is_dma or self.ops[d].eng != eng]
                b.rd.append(o.id)
        return o

    def dma(self, queue, out, in_, sem_buf=None, **kw):
        rb, wb = [], []
        if isinstance(in_, V):
            rb.append(in_.buf)
            in_ap = in_.ap
        else:
            in_ap = in_
        if isinstance(out, V):
            wb.append(out.buf)
            out_ap = out.ap
        else:
            out_ap = out
        if sem_buf is None:
            sem_buf = wb[0] if wb else rb[0]
        if sem_buf.dsem is None:
            sem_buf.dsem = len(self.dma_sems)
            self.dma_sems.append(self.new_sem("d_" + sem_buf.name))
        o = Op()
        o.eng = queue
        o.is_dma = True
        o.deps = self._dma_deps(rb, wb, sem_buf.dsem)
        sem_buf.dcnt += 16
        o.sem = sem_buf.dsem
        o.val = sem_buf.dcnt
        o.milestone = True
        o.id = len(self.ops)
        o.fn = (lambda e, oa=out_ap, ia=in_ap, kw=kw: e.dma_start(out=oa, in_=ia, **kw))
        self.ops.append(o)
        for b in wb:
            if b.lw and all(self.ops[d].is_dma and self.ops[d].sem == o.sem for d in b.lw) and not b.rd:
                b.lw = b.lw + [o.id]
            else:
                b.lw = [o.id]
            b.rd = []
        for b in rb:
            b.rd.append(o.id)
        return o

    def dma_raw(self, queue, fn, reads=(), writes=(), sem_buf=None):
        rb = [(r.buf if isinstance(r, V) else r) for r in reads]
        wb = [(w.buf if isinstance(w, V) else w) for w in writes]
        if sem_buf is None:
            sem_buf = wb[0] if wb else rb[0]
        if sem_buf.dsem is None:
            sem_buf.dsem = len(self.dma_sems)
            self.dma_sems.append(self.new_sem("d_" + sem_buf.name))
        o = Op()
        o.eng = queue
        o.is_dma = True
        o.deps = self._dma_deps(rb, wb, sem_buf.dsem)
        sem_buf.dcnt += 16
        o.sem = sem_buf.dsem
        o.val = sem_buf.dcnt
        o.milestone = True
        o.id = len(self.ops)
        o.fn = fn
        self.ops.append(o)
        for b in wb:
            if b.lw and all(self.ops[d].is_dma and self.ops[d].sem == o.sem for d in b.lw) and not b.rd:
                b.lw = b.lw + [o.id]
            else:
                b.lw = [o.id]
            b.rd = []
        for b in rb:
            b.rd.append(o.id)
        return o

    def finish(self, final_wait_engine=SP):
        nc = self.nc
        ops = self.ops
        for o in ops:
            for d in o.deps:
                p = ops[d]
                if not p.is_dma:
                    if p.eng == PE and o.eng == PE and not o.is_dma:
                        continue
                    p.milestone = True
        eng_sem = {}
        for e in COMPUTE:
            eng_sem[e] = self.new_sem("e_" + e)
        cnt = {e: 0 for e in COMPUTE}
        for o in ops:
            if not o.is_dma and o.milestone:
                cnt[o.eng] += 1
                o.sem = ("E", o.eng)
                o.val = cnt[o.eng]
        streams = {e: [] for e in (PE, ACT, DVE, POOL, SP)}
        known = {e: {} for e in streams}
        nwait = 0
        for o in ops:
            need = {}
            for d in o.deps:
                p = ops[d]
                if (not p.is_dma) and p.eng == PE and o.eng == PE and not o.is_dma:
                    continue
                key = p.sem
                if need.get(key, 0) < p.val:
                    need[key] = p.val
            kn = known[o.eng]
            for key, val in need.items():
                if kn.get(key, 0) >= val:
                    continue
                kn[key] = val
                streams[o.eng].append(("w", key, val))
                nwait += 1
            streams[o.eng].append(("o", o))
        fin = []
        for o in ops:
            pass
        dma_final = {}
        for o in ops:
            if o.is_dma:
                dma_final[o.sem] = max(dma_final.get(o.sem, 0), o.val)
        for key, val in dma_final.items():
            if known[final_wait_engine].get(key, 0) < val:
                streams[final_wait_engine].append(("w", key, val))
        for e in COMPUTE:
            if cnt[e] > 0:
                streams[final_wait_engine].append(("w", ("E", e), cnt[e]))
        self.stats = dict(n_ops=len(ops), n_wait=nwait,
                          per_eng={e: sum(1 for s in streams[e] if s[0] == "o") for e in streams},
                          n_dma_sems=len(self.dma_sems))

        def semh(key):
            if isinstance(key, tuple):
                return eng_sem[key[1]]
            return self.dma_sems[key]

        def replay(e, engobj):
            for s in streams[e]:
                if s[0] == "w":
                    engobj.wait_ge(semh(s[1]), s[2])
                else:
                    o = s[1]
                    ins = o.fn(engobj)
                    if o.is_dma:
                        ins.then_inc(self.dma_sems[o.sem], 16)
                    elif o.milestone:
                        ins.then_inc(eng_sem[o.eng], 1)

        with nc.Block() as block:
            @block.tensor
            def _(e):
                replay(PE, e)

            @block.scalar
            def _(e):
                replay(ACT, e)

            @block.vector
            def _(e):
                replay(DVE, e)

            @block.gpsimd
            def _(e):
                replay(POOL, e)

            @block.sync
            def _(e):
                replay(SP, e)
        self.es.close()
        return nc

    def mm(self, out, lhsT, rhs, start=True, stop=True, **kw):
        return self.op(PE, lambda e: e.matmul(out.ap, lhsT.ap, rhs.ap, start=start, stop=stop, **kw),
                       reads=[lhsT, rhs], writes=[out])

    def transpose(self, out, in_, ident):
        return self.op(PE, lambda e: e.transpose(out.ap, in_.ap, ident.ap),
                       reads=[in_, ident], writes=[out])

    def act(self, out, in_, func, bias=None, scale=None, accum_out=None, eng=ACT):
        kw = {}
        rd = [in_]
        if bias is not None:
            kw["bias"] = bias.ap if isinstance(bias, V) else bias
            if isinstance(bias, V):
                rd.append(bias)
        if scale is not None:
            kw["scale"] = scale.ap if isinstance(scale, V) else scale
            if isinstance(scale, V):
                rd.append(scale)
        wr = [out]
        if accum_out is not None:
            kw["accum_out"] = accum_out.ap
            wr.append(accum_out)
        return self.op(ACT, lambda e: e.activation(out.ap, in_.ap, func, **kw), reads=rd, writes=wr)

    def tt(self, eng, out, in0, in1, op):
        return self.op(eng, lambda e: e.tensor_tensor(out.ap, in0.ap, in1.ap, op),
                       reads=[in0, in1], writes=[out])

    def ts(self, eng, out, in0, s1, op0, s2=None, op1=None, accum_out=None):
        rd = [in0]
        a1 = s1.ap if isinstance(s1, V) else s1
        if isinstance(s1, V):
            rd.append(s1)
        a2 = s2.ap if isinstance(s2, V) else s2
        if isinstance(s2, V):
            rd.append(s2)
        wr = [out]
        kw = {}
        if accum_out is not None:
            kw["accum_out"] = accum_out.ap
            wr.append(accum_out)
        if op1 is None:
            return self.op(eng, lambda e: e.tensor_scalar(out.ap, in0.ap, a1, None, op0, **kw),
                           reads=rd, writes=wr)
        return self.op(eng, lambda e: e.tensor_scalar(out.ap, in0.ap, a1, a2, op0, op1, **kw),
                       reads=rd, writes=wr)

    def stt(self, eng, out, in0, scalar, in1, op0, op1):
        rd = [in0, in1]
        a = scalar.ap if isinstance(scalar, V) else scalar
        if isinstance(scalar, V):
            rd.append(scalar)
        return self.op(eng, lambda e: e.scalar_tensor_tensor(out.ap, in0.ap, a, in1.ap, op0, op1),
                       reads=rd, writes=[out])

    def copy(self, eng, out, in_):
        if eng == ACT:
            return self.op(ACT, lambda e: e.copy(out.ap, in_.ap), reads=[in_], writes=[out])
        return self.op(eng, lambda e: e.tensor_copy(out.ap, in_.ap), reads=[in_], writes=[out])

    def memset(self, eng, out, val):
        return self.op(eng, lambda e: e.memset(out.ap, val), reads=[], writes=[out])


from concourse.bass_utils import run_bass_kernel_spmd

D = 1024
DEPTH = 4
ALPHA = (2.0 * DEPTH) ** 0.25
EPS = 1e-6
NG = 512
NW = 3
SCALE = (128 + 64) ** -0.5


def build_program(mixers=(0, 1, 2), depth=DEPTH, dbg=False):
    nc = bass.Bass("TRN2", target_bir_lowering=False)
    k = KB(nc)
    xp = k.dram_in("xp", [2048, D])
    xsm = k.dram_in("xs", [16, D])
    cT = k.dram_in("cT", [D, 5])
    w_ada = k.dram_in("w_ada", [depth, D, 6 * D])
    b_adaT = k.dram_in("b_adaT", [128, DEPTH, 48])
    lnv = k.dram_in("lnv", [128, 4, DEPTH, 8])
    ffn_w1 = k.dram_in("ffn_w1", [depth, D, 4 * D])
    ffn_w2 = k.dram_in("ffn_w2", [depth, 4 * D, D])
    a_w_in = k.dram_in("a_w_in", [2, D, 2 * D])
    a_ln_g = k.dram_in("a_ln_g", [2, D])
    a_ln_b = k.dram_in("a_ln_b", [2, D])
    a_w_s = k.dram_in("a_w_s", [2, 8, 128, 128])
    a_b_s = k.dram_in("a_b_s", [2, 8 * 128])
    a_w_out = k.dram_in("a_w_out", [2, D, D])
    b_w_in = k.dram_in("b_w_in", [1, D, 4 * D])
    b_w_out = k.dram_in("b_w_out", [1, D, D])
    b_lbT = k.dram_in("b_lbT", [128, 4, 8])
    state_in = k.dram_in("state_in", [4, 8, 128, 128])
    m0pd = k.dram_in("m0p", [128, NG])
    m0sd = k.dram_in("m0s", [128, 16])
    c_w_in = k.dram_in("c_w_in", [1, D, 832])
    c_g_qT = k.dram_in("c_g_qT", [128, 4])
    c_g_kv = k.dram_in("c_g_kv", [1, 256])
    c_w_uq = k.dram_in("c_w_uq", [1, 512, 1536])
    c_w_uk = k.dram_in("c_w_uk", [1, 256, 8, 128])
    c_w_uv = k.dram_in("c_w_uv", [1, 256, 8, 128])
    c_w_out = k.dram_in("c_w_out", [1, D, D])
    cache_lat = k.dram_in("cache_lat", [5120 * 16, 2048])
    cache_rope = k.dram_in("cache_rope", [5120 * 16, 512])
    ptT = k.dram_in("ptT", [128, 4], I32)
    blkd = k.dram_in("blkf", [128, 16])
    nmaskd = k.dram_in("nmask", [16, 4, 32])
    cosd = k.dram_in("cosT", [64, 2064])
    sind = k.dram_in("sinT", [64, 2064])
    rmd = k.dram_in("rotm", [64, 64])
    identd = k.dram_in("ident", [128, 128])
    trild = k.dram_in("tril", [128, 128])
    bmaskd = k.dram_in("bmask", [16, 16])
    y_p = k.dram_out("y_p", [2048, D])
    y_s = k.dram_out("y_s", [16, D])
    v_s = k.dram_out("v_s", [2, 16, D])
    hs_p = k.dram_out("hs_p", [8, 128, 128])
    hs_s = k.dram_out("hs_s", [4, 8, 128, 128])
    lat_p = k.dram_out("lat_p", [2048, 256])
    rope_p = k.dram_out("rope_p", [2048, 64])
    lat_s = k.dram_out("lat_s", [16, 256])
    rope_s = k.dram_out("rope_s", [16, 64])
    dbg_o = k.dram_out("dbg_x", [depth, 2064, D]) if dbg else None

    ident = k.sb("identf", [128, 128])
    onesf = k.sb("onesf", [128, 128])
    tril = k.sb("trilf", [128, 128])
    bmask = k.sb("bmaskf", [16, 16])
    epsc = k.sb("epsc", [128, 1])
    scT = k.sb("scT", [128, 8, 5], BF16)
    scf = k.sb("scf", [128, 8, 5])
    badaT = k.sb("badaT", [128, DEPTH, 48])
    lnc = k.sb("lnc", [128, 4, DEPTH, 8])
    lncA = k.sb("lncA", [128, 4, DEPTH, 8])
    modD = k.sb("modD", [128, DEPTH, 48, 5])
    modS = k.sb("modS", [128, 6, 8, 16])
    modG2 = k.sb("modG2", [128, DEPTH, 2, 8])
    modB2 = k.sb("modB2", [128, DEPTH, 2, 8])
    xT = k.sb("xT", [128, 8, NG])
    hT = k.sb("hT", [128, 8, NG], BF16)
    ring = [k.sb(f"wr{i}", [128, 4096], BF16) for i in range(NW)]
    arena = [k.sb(f"ar{i}", [128, 4096], BF16) for i in range(10)]
    identb = k.sb("identb", [128, 128], BF16)
    Sst = k.sb("Sst", [128, 8, 128])
    Sbf = k.sb("Sbf", [128, 8, 128], BF16)
    lbe = k.sb("lbe", [128, 4, 8])
    lbc = k.sb("lbc", [128, 8])
    omlc = k.sb("omlc", [128, 8])
    lbs = k.sb("lbs", [128, 8])
    decT = k.sb("decT", [128, 8, 8])
    m0p = k.sb("m0p_sb", [128, NG])
    m0s = k.sb("m0s_sb", [128, 16])
    Klsb = [k.sb(f"Klsb{i}", [64, 128], BF16) for i in range(2)]
    ATsb = [k.sb(f"ATsb{i}", [64, 64], BF16) for i in range(2)]
    bss = k.sb("bss", [64, 8])
    brs = k.sb("brs", [64, 8])
    KVt = k.sb("KVt", [128, 16, 256], BF16)
    KTl = k.sb("KTl", [128, 2, 2048], BF16)
    KTr = k.sb("KTr", [64, 2048], BF16)
    WukT = k.sb("WukT", [128, 8, 256], BF16)
    Wuv = k.sb("Wuv", [128, 2, 8, 128], BF16)
    gqc = k.sb("gqc", [128, 4])
    gkvb = k.sb("gkvb", [128, 256])
    rotm = k.sb("rotm_sb", [64, 64])
    onesb = k.sb("onesb", [128, 128], BF16)
    cosg = k.sb("cosg", [64, NG])
    sing = k.sb("sing", [64, NG])
    PTb = [k.sb(f"PTb{i}", [128, NG], BF16) for i in range(2)]
    olat = k.sb("olat", [128, 2, NG], BF16)
    rsum = k.sb("rsum", [128, NG])
    onb = V(olat.ap.rearrange("p c n -> p (c n)")[0:64, :].rearrange("p (h v) -> p h v", h=8), olat)
    css = k.sb("css", [128, 4])
    pti = k.sb("pti", [128, 4], I32)
    ptf = k.sb("ptf", [128, 4])
    blkf = k.sb("blkf_sb", [128, 16])
    idxf = k.sb("idxf", [128, 4, 16])
    idxi = k.sb("idxi", [128, 4, 16], I32)
    nmask = k.sb("nmask_sb", [16, 4, 32])
    PTs = PTb
    KTs = [V(olat.ap.rearrange("p c n -> p (c n)")[:, 0:384].rearrange("p (c n) -> p c n", c=3), olat),
           V(rsum.ap.bitcast(BF16)[:, 0:384].rearrange("p (c n) -> p c n", c=3), rsum)]
    dsm = k.sb("dsm", [32, 4])
    tmpA = [k.sb(f"tmpA{i}", [128, NG]) for i in range(2)]
    rl = [k.sb(f"rl{i}", [128, NG], BF16) for i in range(2)]
    xin0 = k.sb("xin0", [128, D])
    xin = [xin0, xin0]
    bnst = k.sb("bnst", [128, 2, 6])
    bnmv = k.sb("bnmv", [128, 4])
    wsT = k.sb("wsT", [128, 8, 128], BF16)
    wsTs = k.sb("wsTs", [16, 8, 16], BF16)
    bsb = k.sb("bsb", [128, 8, 128])
    bsbs = k.sb("bsbs", [128, 8, 16])
    PS = [k.ps(f"ps{i}", [128, 512]) for i in range(8)]
    st = {"ri": 0, "pi": 0}

    bcache = {}

    def bcreg(e):
        if "r" not in bcache:
            cm = e.register("bcreg")
            bcache["r"] = cm.__enter__()
            e.reg_mov(bcache["r"], 5120 * 16 - 1)
        return bcache["r"]

    def oq():
        return SP if st.get("first", True) else POOL

    def pnext():
        p = PS[st["pi"] % 4]
        st["pi"] += 1
        return p

    NSCR = 96
    scr = nc.dram_tensor("wscratch", [NSCR, 128, 4096], BF16).ap()
    scrbuf = Buf(scr, "wscratch")
    wcache = {}

    def wload(src, a, b, key=None):
        buf = ring[st["ri"] % NW]
        st["ri"] += 1
        flat = buf.v[:, 0:a * b]
        view = flat.rearrange("p (a b) -> p a b", a=a)
        if key is not None and key in wcache:
            t = wcache[key]
            k.dma(SP, flat, V(scr[t][:, 0:a * b], scrbuf), sem_buf=buf)
            return view
        k.dma(POOL, view, src)
        if key is not None and len(wcache) < NSCR:
            t = len(wcache)
            wcache[key] = t
            k.dma(SP, V(scr[t][:, 0:a * b], scrbuf), flat, sem_buf=buf)
        return view

    def wblk(w2d, blk, width=512, key=None):
        kc = w2d.shape[0] // 128
        return wload(w2d[:, blk * width:(blk + 1) * width].rearrange("(c p) n -> p c n", p=128), kc, width,
                     key=None if key is None else (key, blk))

    k.dma(SP, ident.v, identd)
    k.copy(DVE, identb.v, ident.v)
    k.dma(SP, m0p.v, m0pd)
    k.dma(SP, m0s.v, m0sd)
    k.dma(SP, lbe.v, b_lbT)
    k.act(lbe.v, lbe.v, AF.Exp)
    k.op(DVE, lambda e: e.tensor_reduce(lbs.ap, lbe.ap.rearrange("p l h -> p h l"), AX.X, ALU.add), reads=[lbe], writes=[lbs])
    k.op(DVE, lambda e: e.reciprocal(lbs.ap, lbs.ap), reads=[lbs], writes=[lbs])
    k.tt(DVE, lbc.v, lbe[:, 1, :], lbs.v, ALU.mult)
    k.ts(DVE, omlc.v, lbc.v, -1.0, ALU.mult, 1.0, ALU.add)
    k.memset(DVE, Sst.v, 0.0)
    k.memset(DVE, Sbf.v, 0.0)
    k.dma(SP, tril.v, trild)
    k.dma(SP, bmask.v, bmaskd)
    k.dma(SP, badaT.v, b_adaT)
    k.dma(SP, lnc.v, lnv)
    k.dma(SP, scf.v, cT.rearrange("(c p) s -> p c s", p=128))
    k.memset(DVE, onesf.v, 1.0)
    k.memset(DVE, onesb.v, 1.0)
    k.dma(SP, gqc.v, c_g_qT)
    k.dma(SP, gkvb.v, c_g_kv.broadcast_to([128, 256]))
    k.dma(SP, rotm.v, rmd)
    k.dma(SP, pti.v, ptT)
    k.dma(SP, blkf.v, blkd)
    k.dma(SP, nmask.v, nmaskd)
    k.copy(DVE, ptf.v, pti.v)
    k.ts(DVE, ptf.v, ptf.v, 16.0, ALU.mult)
    k.tt(DVE, idxf.v, V(ptf.ap.unsqueeze(2).broadcast_to([128, 4, 16]), ptf),
         V(blkf.ap.unsqueeze(1).broadcast_to([128, 4, 16]), blkf), ALU.add)
    k.copy(DVE, idxi.v, idxf.v)
    k.memset(DVE, epsc.v, EPS)
    k.act(scT.v, scf.v, AF.Silu)
    k.op(ACT, lambda e: e.mul(lncA.ap, lnc.ap, ALPHA), reads=[lnc], writes=[lncA])

    def ensure_mods(l):
        if l >= depth or l in st["mods"]:
            return
        st["mods"].add(l)
        for blk in range(12):
            wt = wblk(w_ada[l], blk)
            for f4 in range(4):
                j = blk * 4 + f4
                ps = pnext()
                for c in range(8):
                    k.mm(ps[:, 0:5], wt[:, c, f4 * 128:(f4 + 1) * 128], scT[:, c, :], start=(c == 0), stop=(c == 7))
                k.act(modD[:, l, j, :], ps[:, 0:5], AF.Identity, bias=badaT[:, l, j:j + 1])
        for kind in (1, 4):
            v = modD[:, l, kind * 8:(kind + 1) * 8, :]
            k.ts(DVE, v, v, 1.0, ALU.add, 1.0 / ALPHA, ALU.mult)

    def ensure_fold(l):
        if l >= depth or l in st["fold"]:
            return
        st["fold"].add(l)
        for which in range(2):
            if which == 0:
                ls, ks, ksh = l, 4, 3
            else:
                if l + 1 >= depth:
                    continue
                ls, ks, ksh = l + 1, 1, 0
            sc_ = modD[:, ls, ks * 8:(ks + 1) * 8, 0]
            sh_ = modD[:, ls, ksh * 8:(ksh + 1) * 8, 0]
            k.tt(DVE, modG2[:, l, which, :], lncA[:, 2 * which, l, :], sc_, ALU.mult)
            k.tt(DVE, modB2[:, l, which, :], lncA[:, 2 * which + 1, l, :], sc_, ALU.mult)
            k.tt(DVE, modB2[:, l, which, :], modB2[:, l, which, :], sh_, ALU.add)

    st["mods"] = set()
    st["fold"] = set()
    ensure_mods(0)
    ensure_mods(1)
    ensure_fold(0)

    groups = [("p", g) for g in range(4)] + [("s", 0)]
    marks = []
    k.marks = marks

    def mark(lbl):
        marks.append((lbl, sum(1 for o in k.ops if o.eng == PE)))

    def set_modS(l):
        for kind in range(6):
            src = modD[:, l, kind * 8:(kind + 1) * 8, 1:5]
            k.copy(DVE, modS[:, kind, :, :].rearrange("p c (s t) -> p c s t", t=4),
                   V(src.ap.unsqueeze(3).broadcast_to([128, 8, 4, 4]), src.buf))

    def modulate(l, ks, ksh, N, samp):
        if (not samp) and st.get("hready") == (l, ks):
            st["hready"] = None
            return
        if not samp:
            for c in range(8):
                k.act(hT[:, c, :N], xT[:, c, :N], AF.Identity, scale=modD[:, l, ks * 8 + c, 0:1],
                      bias=modD[:, l, ksh * 8 + c, 0:1])
        else:
            t = arena[6].v.bitcast(F32)[:, 0:128].rearrange("p (c n) -> p c n", c=8)
            k.tt(DVE, t, xT[:, :, :N], modS[:, ks, :, :], ALU.mult)
            k.tt(DVE, hT[:, :, :N], t, modS[:, ksh, :, :], ALU.add)

    ysq_ = arena[4].v.rearrange("p (c n) -> p c n", c=8)
    ybf_ = arena[5].v.rearrange("p (c n) -> p c n", c=8)
    st["lnq"] = []
    st["lnn"] = 0

    def ln_stats_mm(c, N):
        n = st["lnn"]
        k.mm(PS[4][:, :N], onesb.v, ybf_[:, c, :N], start=(n == 0), stop=(n == 7))
        k.mm(PS[5][:, :N], onesb.v, ysq_[:, c, :N], start=(n == 0), stop=(n == 7))
        st["lnn"] = n + 1

    def residual(ps, c, l, kg, N, samp):
        if not samp:
            q = st["lnq"]
            lag = 1 if kg == 5 else 3
            while len(q) >= lag:
                ln_stats_mm(q.pop(0), N)
            k.stt(DVE, xT[:, c, :N], ps[:, :N], modD[:, l, kg * 8 + c, 0:1], xT[:, c, :N], ALU.mult, ALU.add)
            k.act(ysq_[:, c, :N], xT[:, c, :N], AF.Square)
            k.copy(DVE, ybf_[:, c, :N], xT[:, c, :N])
            q.append(c)
        else:
            t = tmpA[c % 2]
            k.tt(DVE, t[:, :N], ps[:, :N], modS[:, kg, c, :], ALU.mult)
            k.tt(DVE, xT[:, c, :N], t[:, :N], xT[:, c, :N], ALU.add)

    tnb = [Buf(None, f"tn{c}") for c in range(8)]

    def layernorm(l, which, N, final, samp):
        ysq = arena[4].v.rearrange("p (c n) -> p c n", c=8)
        ybf = arena[5].v.rearrange("p (c n) -> p c n", c=8)
        tn = [V(arena[7 + c // 4].ap.bitcast(F32)[:, (c % 4) * NG:(c % 4) * NG + N], tnb[c]) for c in range(8)]
        abuf = [arena[7 + c // 4] for c in range(8)]
        ps_s, ps_q = PS[4], PS[5]
        if st["lnn"] + len(st["lnq"]) == 8:
            while st["lnq"]:
                ln_stats_mm(st["lnq"].pop(0), N)
        else:
            assert st["lnn"] == 0 and not st["lnq"]
            for c in range(8):
                k.act(ysq[:, c, :N], xT[:, c, :N], AF.Square)
                k.copy(DVE, ybf[:, c, :N], xT[:, c, :N])
            for c in range(8):
                k.mm(ps_s[:, :N], onesb.v, ybf[:, c, :N], start=(c == 0), stop=(c == 7))
            for c in range(8):
                k.mm(ps_q[:, :N], onesb.v, ysq[:, c, :N], start=(c == 0), stop=(c == 7))
        st["lnn"] = 0
        stt_ = V(arena[6].ap.bitcast(F32), arena[6])
        mean, msq, sd, rstd = (stt_[:, i * NG:i * NG + N] for i in range(4))
        k.ts(DVE, mean, ps_s[:, :N], 1.0 / D, ALU.mult)
        k.tt(DVE, msq, mean, mean, ALU.mult)
        k.stt(DVE, msq, ps_q[:, :N], 1.0 / D, msq, ALU.mult, ALU.subtract)
        k.act(sd, msq, AF.Sqrt, bias=epsc.v)
        gsrc = lnc if final else lncA
        fuse = (not samp) and (not final) and (which == 0 or l + 1 < depth)
        for c in range(8):
            t = tn[c]
            k.op(DVE, lambda e, t=t, c=c: e.tensor_tensor(t.ap, xT.ap[:, c, :N], mean.ap, ALU.subtract),
                 reads=[xT, mean, abuf[c]], writes=[t])
        k.op(DVE, lambda e: e.reciprocal(rstd.ap, sd.ap), reads=[sd], writes=[rstd])
        for c in range(8):
            t = tn[c]
            k.op(DVE, lambda e, t=t: e.tensor_tensor(t.ap, t.ap, rstd.ap, ALU.mult), reads=[t, rstd, abuf[c]], writes=[t])
        if fuse:
            for c in range(8):
                t = tn[c]
                k.op(ACT, lambda e, t=t, c=c: e.activation(hT.ap[:, c, :N], t.ap, AF.Identity,
                                                            scale=modG2.ap[:, l, which, c:c + 1],
                                                            bias=modB2.ap[:, l, which, c:c + 1]),
                     reads=[t, modG2, modB2, abuf[c]], writes=[hT])
        for c in range(8):
            t = tn[c]
            k.op(ACT, lambda e, t=t, c=c: e.activation(xT.ap[:, c, :N], t.ap, AF.Identity,
                                                        scale=gsrc.ap[:, 2 * which, l, c:c + 1],
                                                        bias=gsrc.ap[:, 2 * which + 1, l, c:c + 1]),
                 reads=[t, gsrc, abuf[c]], writes=[xT])
        st["hready"] = (l, 4) if (fuse and which == 0) else ((l + 1, 1) if fuse else None)

    def ffn(l, N, samp):
        modulate(l, 4, 3, N, samp)
        hid = [arena[i].v.rearrange("p (c n) -> p c n", c=8) for i in range(4)]
        for blk in range(8):
            wt = wblk(ffn_w1[l], blk, key=("w1", l))
            for f4 in range(4):
                f = blk * 4 + f4
                ps = pnext()
                for c in range(8):
                    k.mm(ps[:, :N], wt[:, c, f4 * 128:(f4 + 1) * 128], hT[:, c, :N], start=(c == 0), stop=(c == 7))
                r = rl[f % 2]
                k.act(r[:, :N], ps[:, :N], AF.Relu)
                k.tt(DVE, hid[f // 8][:, f % 8, :N], r[:, :N], r[:, :N], ALU.mult)
        for d in range(8):
            wt = wload(ffn_w2[l][:, d * 128:(d + 1) * 128].rearrange("(c p) n -> p c n", p=128), 32, 128, key=("w2", l, d))
            ps = pnext()
            for c in range(32):
                k.mm(ps[:, :N], wt[:, c, :], hid[c // 8][:, c % 8, :N], start=(c == 0), stop=(c == 31))
            residual(ps, d, l, 5, N, samp)

    def prep_A(j):
        wsf = arena[6].v.bitcast(F32)[:, 0:1024].rearrange("p (g s) -> p g s", g=8)
        k.dma(oq(), wsf, a_w_s[j].rearrange("g t s -> t g s"))
        for g in range(8):
            ps = PS[6 + g % 2]
            k.transpose(ps[:, 0:128], wsf[:, g, :], ident.v)
            k.tt(DVE, wsT[:, g, :], ps[:, 0:128], tril.v, ALU.mult)
        k.dma(oq(), bsb.v.rearrange("p g t -> p (g t)"), a_b_s[j:j + 1, :].broadcast_to([128, 1024]))
        k.memset(DVE, wsTs.v, 0.0)
        for b in range(4):
            k.dma(oq(), wsTs[4 * b:4 * b + 4, :, 4 * b:4 * b + 4], wsT[0:4, :, 0:4])
        k.copy(DVE, bsbs.v.rearrange("p g (s t) -> p g s t", t=4),
               V(bsb.ap[:, :, 0:4].unsqueeze(2).broadcast_to([128, 8, 4, 4]), bsb))

    def mixer_A(l, j, N, samp):
        prep_A(j)
        modulate(l, 1, 0, N, samp)
        nt = 16 if samp else 128
        nch = 1 if samp else N // 128
        uT = arena[0].v.rearrange("p (c n) -> p c n", c=8)
        oT = arena[1].v.rearrange("p (c n) -> p c n", c=8)
        vf = [V(arena[2].ap.bitcast(F32), arena[2]), V(arena[3].ap.bitcast(F32), arena[3])]
        vln = arena[4].v.rearrange("p (c n) -> p c n", c=4)
        vsout = V(arena[7].ap.bitcast(F32)[0:16, 0:1024], arena[7])
        lng = V(arena[5].ap.bitcast(F32)[:, 0:1024], arena[5])
        lnb = V(arena[5].ap.bitcast(F32)[:, 1024:2048], arena[5])
        k.dma(oq(), lng, a_ln_g[j:j + 1, :].broadcast_to([128, 1024]))
        k.dma(oq(), lnb, a_ln_b[j:j + 1, :].broadcast_to([128, 1024]))
        for blk in range(2):
            wt = wblk(a_w_in[j], blk, key=("a_in", j))
            for f4 in range(4):
                f = blk * 4 + f4
                ps = pnext()
                for c in range(8):
                    k.mm(ps[:, :N], wt[:, c, f4 * 128:(f4 + 1) * 128], hT[:, c, :N], start=(c == 0), stop=(c == 7))
                k.act(uT[:, f, :N], ps[:, :N], AF.Gelu_apprx_tanh)
        def vch(ch):
            return vf[ch // 2][:, (ch % 2) * 1024:(ch % 2) * 1024 + 1024]
        for blk in range(2, 4):
            wt = wblk(a_w_in[j], blk, key=("a_in", j))
            for ch in range(nch):
                ps = pnext()
                for c in range(8):
                    k.mm(ps[0:nt, :], hT[:, c, ch * 128:ch * 128 + nt], wt[:, c, :], start=(c == 0), stop=(c == 7))
                k.act(vch(ch)[0:nt, (blk - 2) * 512:(blk - 1) * 512], ps[0:nt, :], AF.Gelu_apprx_tanh)
        for ch in range(nch):
            v = vch(ch)
            for hh in range(2):
                k.op(DVE, lambda e, hh=hh, v=v: e.bn_stats(bnst.ap[0:nt, hh, :], v.ap[0:nt, hh * 512:(hh + 1) * 512]),
                     reads=[v], writes=[bnst])
            k.op(DVE, lambda e: e.bn_aggr(bnmv.ap[0:nt, 0:2], bnst.ap[0:nt].rearrange("p a b -> p (a b)")),
                 reads=[bnst], writes=[bnmv])
            k.act(bnmv[0:nt, 2:3], bnmv[0:nt, 1:2], AF.Sqrt, bias=epsc[0:nt, :])
            k.op(DVE, lambda e: e.reciprocal(bnmv.ap[0:nt, 3:4], bnmv.ap[0:nt, 2:3]), reads=[bnmv], writes=[bnmv])
            k.ts(DVE, v[0:nt, :], v[0:nt, :], bnmv[0:nt, 0:1], ALU.subtract, bnmv[0:nt, 3:4], ALU.mult)
            k.tt(DVE, v[0:nt, :], v[0:nt, :], lng[0:nt, :], ALU.mult)
            if samp:
                k.tt(DVE, vsout, v[0:nt, :], lnb[0:nt, :], ALU.add)
                k.dma(oq(), v_s[j], vsout)
                k.copy(DVE, vln[0:nt, ch, :], vsout)
            else:
                k.tt(DVE, vln[0:nt, ch, :], v[0:nt, :], lnb[0:nt, :], ALU.add)
        for g in range(8):
            ps = pnext()
            for ch in range(nch):
                if samp:
                    k.mm(ps[:, 0:16], vln[0:16, 0, g * 128:(g + 1) * 128], wsTs[:, g, :])
                else:
                    k.mm(ps[:, ch * 128:(ch + 1) * 128], vln[:, ch, g * 128:(g + 1) * 128], wsT[:, g, :])
            t = tmpA[g % 2]
            if samp:
                k.tt(DVE, t[:, :N], ps[:, :N], bsbs[:, g, :], ALU.add)
            else:
                k.tt(DVE, t[:, :N].rearrange("p (c t) -> p c t", t=128), ps[:, :N].rearrange("p (c t) -> p c t", t=128),
                     V(bsb.ap[:, g, :].unsqueeze(1).broadcast_to([128, nch, 128]), bsb), ALU.add)
            k.tt(DVE, oT[:, g, :N], t[:, :N], uT[:, g, :N], ALU.mult)
        for blk in range(2):
            wt = wblk(a_w_out[j], blk, key=("a_out", j))
            for d4 in range(4):
                d = blk * 4 + d4
                ps = pnext()
                for c in range(8):
                    k.mm(ps[:, :N], wt[:, c, d4 * 128:(d4 + 1) * 128], oT[:, c, :N], start=(c == 0), stop=(c == 7))
                residual(ps, d, l, 2, N, samp)


    def mixer_B(l, N, samp):
        modulate(l, 1, 0, N, samp)
        C = 4 if samp else 64
        nchk = N // C
        mid, last = (1, 3) if samp else (31, 63)
        m0 = m0s if samp else m0p
        def fm(i):
            return arena[i].v.rearrange("p (c n) -> p c n", c=8)
        qT, QbT, QmT, KmT, KlT, sgT = fm(0), fm(1), fm(2), fm(3), fm(4), fm(5)
        ogT = fm(0)
        vtm = [arena[6].v.rearrange("p (c n) -> p c n", c=4), arena[7].v.rearrange("p (c n) -> p c n", c=4)]
        def vch(ci):
            return vtm[ci // 4][0:C, ci % 4, :]
        t8 = V(arena[8].ap.bitcast(F32), arena[8])
        t9 = V(arena[9].ap.bitcast(F32), arena[9])
        tset = [[t8[:, i * NG:i * NG + N] for i in range(4)] + [tmpA[0][:, :N]],
                [t9[:, i * NG:i * NG + N] for i in range(4)] + [tmpA[1][:, :N]]]
        osq = xin0[0:64, :]
        W = b_w_in[0]
        for blk in range(8):
            wt = wblk(W, blk, key=("b_in",))
            if blk in (4, 5):
                for ci in range(nchk):
                    ps = pnext()
                    for c in range(8):
                        k.mm(ps[0:C, :], hT[:, c, ci * C:(ci + 1) * C], wt[:, c, :], start=(c == 0), stop=(c == 7))
                    k.copy(ACT, vch(ci)[:, (blk - 4) * 512:(blk - 3) * 512], ps[0:C, :])
                continue
            for f4 in range(4):
                f = blk * 4 + f4
                h = f % 8
                ps = pnext()
                for c in range(8):
                    k.mm(ps[:, :N], wt[:, c, f4 * 128:(f4 + 1) * 128], hT[:, c, :N], start=(c == 0), stop=(c == 7))
                if blk < 2:
                    k.act(qT[:, h, :N], ps[:, :N], AF.Silu)
                elif blk >= 6:
                    k.act(sgT[:, h, :N], ps[:, :N], AF.Silu)
                else:
                    tf, tk, tb, td, te = tset[h % 2]
                    k.act(tf, ps[:, :N], AF.Sigmoid)
                    k.ts(DVE, tf, tf, omlc[:, h:h + 1], ALU.mult, lbc[:, h:h + 1], ALU.add)
                    k.ts(DVE, tk, tf, -1.0, ALU.mult, 1.0, ALU.add)
                    k.act(tf, tf, AF.Ln)
                    k.op(DVE, lambda e, tb=tb, tlog=tf, m0=m0, N=N: e.tensor_tensor_scan(
                        tb.ap, m0.ap[:, :N], tlog.ap, 0.0, ALU.mult, ALU.add), reads=[m0, tf], writes=[tb])
                    b3 = tb.rearrange("p (a b) -> p a b", b=C)
                    d3 = td.rearrange("p (a b) -> p a b", b=C)
                    k.act(te, tb, AF.Exp)
                    k.tt(DVE, QbT[:, h, :N], qT[:, h, :N], te, ALU.mult)
                    k.tt(DVE, d3, b3, V(b3.ap[:, :, mid:mid + 1].broadcast_to([128, nchk, C]), b3.buf), ALU.subtract)
                    k.act(te, td, AF.Exp)
                    k.tt(DVE, QmT[:, h, :N], qT[:, h, :N], te, ALU.mult)
                    k.act(te, td, AF.Exp, scale=-1.0)
                    k.tt(DVE, KmT[:, h, :N], tk, te, ALU.mult)
                    k.tt(DVE, d3, V(b3.ap[:, :, last:last + 1].broadcast_to([128, nchk, C]), b3.buf), b3, ALU.subtract)
                    k.act(te, td, AF.Exp)
                    k.tt(DVE, KlT[:, h, :N], tk, te, ALU.mult)
                    k.act(decT[:, h, 0:nchk], b3[:, :, last], AF.Exp)
        for ci in range(nchk):
            if samp:
                k.dma(oq(), Sst.v, state_in[ci].rearrange("h k v -> k h v"))
                k.copy(ACT, Sbf.v, Sst.v)
            cs = slice(ci * C, (ci + 1) * C)
            def stage1(h):
                psT, psA = PS[6 + h % 2], (PS[4], PS[3])[h % 2]
                klsb, atsb = Klsb[h % 2], ATsb[h % 2]
                k.transpose(psT.v.bitcast(BF16)[0:C, 0:128], KlT[:, h, cs], identb.v)
                k.copy(ACT, klsb[0:C, :], psT.v.bitcast(BF16)[0:C, 0:128])
                k.mm(psA[0:C, 0:C], KmT[:, h, cs], QmT[:, h, cs])
                k.tt(DVE, atsb[0:C, 0:C], psA[0:C, 0:C], tril[0:C, 0:C], ALU.mult)
            def stage2(h):
                psS = PS[5]
                po = PS[h // 4]
                klsb, atsb = Klsb[h % 2], ATsb[h % 2]
                oc = slice((h % 4) * 128, (h % 4 + 1) * 128)
                k.mm(po[0:C, oc], atsb[0:C, 0:C], vch(ci)[:, h * 128:(h + 1) * 128], start=True, stop=False)
                k.mm(po[0:C, oc], QbT[:, h, cs], Sbf[:, h, :], start=False, stop=True)
                k.mm(psS[:, 0:128], klsb[0:C, :], vch(ci)[:, h * 128:(h + 1) * 128])
                k.stt(DVE, Sst[:, h, :], Sst[:, h, :], decT[:, h, ci:ci + 1], psS[:, 0:128], ALU.mult, ALU.add)
                k.copy(ACT, Sbf[:, h, :], Sst[:, h, :])
            stage1(0)
            for h in range(8):
                if h + 1 < 8:
                    stage1(h + 1)
                stage2(h)
            if samp:
                k.dma(oq(), hs_s[ci].rearrange("h k v -> k h v"), Sst.v)
            for hh in range(2):
                k.act(osq[0:C, hh * 512:(hh + 1) * 512], PS[hh][0:C, :], AF.Square)
            k.op(DVE, lambda e, osq=osq, C=C: e.tensor_reduce(bss.ap[0:C, :], osq.ap[0:C, :].rearrange("p (h v) -> p h v", h=8),
                                                               AX.X, ALU.add), reads=[osq], writes=[bss])
            k.act(brs[0:C, :], bss[0:C, :], AF.Sqrt, bias=epsc[0:C, :], scale=1.0 / 128)
            k.op(DVE, lambda e, C=C: e.reciprocal(brs.ap[0:C, :], brs.ap[0:C, :]), reads=[brs], writes=[brs])
            for hh in range(2):
                k.tt(DVE, onb[0:C, hh * 4:(hh + 1) * 4, :], PS[hh][0:C, :].rearrange("p (h v) -> p h v", h=4),
                     V(brs.ap[0:C, hh * 4:(hh + 1) * 4].unsqueeze(2).broadcast_to([C, 4, 128]), brs), ALU.mult)
            for h in range(8):
                psT = PS[6 + h % 2]
                k.transpose(psT.v.bitcast(BF16)[:, 0:C], onb[0:C, h, :], identb[0:C, 0:C])
                k.tt(DVE, ogT[:, h, cs], psT.v.bitcast(BF16)[:, 0:C], sgT[:, h, cs], ALU.mult)
        for blk in range(2):
            wt = wblk(b_w_out[0], blk, key=("b_out",))
            for d4 in range(4):
                d = blk * 4 + d4
                ps = pnext()
                for c in range(8):
                    k.mm(ps[:, :N], wt[:, c, d4 * 128:(d4 + 1) * 128], ogT[:, c, :N], start=(c == 0), stop=(c == 7))
                residual(ps, d, l, 2, N, samp)


    def prep_C():
        k.dma(POOL, Wuv.v, c_w_uv[0].rearrange("(cc p) h v -> p cc h v", p=128))
        wk = arena[9].v[:, 0:2048].rearrange("p (cc h d) -> p cc h d", cc=2, h=8)
        k.dma(POOL, wk, c_w_uk[0].rearrange("(cc p) h d -> p cc h d", p=128))
        for cc in range(2):
            for h in range(8):
                psT = PS[6 + h % 2]
                k.transpose(psT.v.bitcast(BF16)[:, 0:128], wk[:, cc, h, :], identb.v)
                k.copy(ACT, WukT[:, h, cc * 128:(cc + 1) * 128], psT.v.bitcast(BF16)[:, 0:128])

    def rope_fm(dst, src_f, N, c0):
        psr = PS[3]
        k.mm(psr[0:64, :N], rotm.v, src_f)
        t1 = V(arena[7].ap.bitcast(F32)[0:64, 0:N], arena[7])
        t2 = V(arena[7].ap.bitcast(F32)[0:64, NG:NG + N], arena[7])
        k.tt(DVE, t1, src_f, cosg[:, :N], ALU.mult)
        k.tt(DVE, t2, psr[0:64, :N], sing[:, :N], ALU.mult)
        k.tt(DVE, dst, t1, t2, ALU.add)

    def mixer_C(l, N, samp, gi):
        modulate(l, 1, 0, N, samp)
        nt = 16 if samp else 128
        nch = 1 if samp else N // 128
        t0 = 2048 if samp else gi * NG
        k.dma(oq(), cosg[:, :N], cosd[:, t0:t0 + N])
        k.dma(oq(), sing[:, :N], sind[:, t0:t0 + N])
        cqT = V(arena[0].ap.bitcast(F32), arena[0]).rearrange("p (c n) -> p c n", c=4)
        sqT = V(arena[1].ap.bitcast(F32), arena[1]).rearrange("p (c n) -> p c n", c=4)
        cqn = arena[2].v[:, 0:2048].rearrange("p (c n) -> p c n", c=4)
        qnsb = arena[2].v[:, 2048:2048 + NG]
        QlT = [arena[3].v.rearrange("p (h n) -> p h n", h=8), arena[4].v.rearrange("p (h n) -> p h n", h=8)]
        QrT = arena[5].v.rearrange("p (h n) -> p h n", h=8)
        attnT = arena[6].v.rearrange("p (h n) -> p h n", h=8)
        a8 = V(arena[8].ap.bitcast(F32), arena[8])
        krf = a8[0:64, 0:N]
        krr = a8[0:64, NG:NG + N]
        klf = a8[:, 2 * NG:2 * NG + 256]
        rstd = a8[:, 3 * NG:3 * NG + N]
        W = c_w_in[0]
        wt = wblk(W, 0, key=("c_in",))
        for f in range(4):
            ps = pnext()
            for c in range(8):
                k.mm(ps[:, :N], wt[:, c, f * 128:(f + 1) * 128], hT[:, c, :N], start=(c == 0), stop=(c == 7))
            k.copy(ACT, cqT[:, f, :N], ps[:, :N])
            k.act(sqT[:, f, :N], ps[:, :N], AF.Square)
        psq = PS[4]
        for f in range(4):
            k.mm(psq[:, :N], onesf.v, sqT[:, f, :N], start=(f == 0), stop=(f == 3))
        k.act(rstd, psq[:, :N], AF.Sqrt, bias=epsc.v, scale=1.0 / 512)
        k.op(DVE, lambda e, rstd=rstd: e.reciprocal(rstd.ap, rstd.ap), reads=[rstd], writes=[rstd])
        for f in range(4):
            k.stt(DVE, cqn[:, f, :N], cqT[:, f, :N], gqc[:, f:f + 1], rstd, ALU.mult, ALU.mult)
        wt = wload(W[:, 512:832].rearrange("(c p) n -> p c n", p=128), 8, 320, key=("c_in2",))
        for ch in range(nch):
            ps = pnext()
            for c in range(8):
                k.mm(ps[0:nt, 0:256], hT[:, c, ch * 128:ch * 128 + nt], wt[:, c, 0:256], start=(c == 0), stop=(c == 7))
            junk = sqT[0:nt, 0, 0:256]
            k.act(junk, ps[0:nt, 0:256], AF.Square, accum_out=css[0:nt, 0:1])
            k.act(css[0:nt, 1:2], css[0:nt, 0:1], AF.Sqrt, bias=epsc[0:nt, :], scale=1.0 / 256)
            k.op(DVE, lambda e, nt=nt: e.reciprocal(css.ap[0:nt, 2:3], css.ap[0:nt, 1:2]), reads=[css], writes=[css])
            k.stt(DVE, klf[0:nt, :], ps[0:nt, 0:256], css[0:nt, 2:3], gkvb[0:nt, :], ALU.mult, ALU.mult)
            if samp:
                k.dma(oq(), lat_s, klf[0:nt, :])
                kvs = arena[9].v[0:16, 0:256]
                k.copy(ACT, kvs, klf[0:nt, :])
            else:
                tile = gi * 4 + ch
                k.dma(oq(), lat_p[tile * 128:(tile + 1) * 128, :], klf[0:nt, :])
                k.copy(ACT, KVt[:, tile, :], klf[0:nt, :])
                for cc in range(2):
                    psT = PS[6 + cc]
                    k.transpose(psT.v.bitcast(BF16)[:, 0:128], KVt[:, tile, cc * 128:(cc + 1) * 128], identb.v)
                    k.copy(ACT, KTl[:, cc, tile * 128:(tile + 1) * 128], psT.v.bitcast(BF16)[:, 0:128])
        ps = pnext()
        for c in range(8):
            k.mm(ps[0:64, :N], wt[:, c, 256:320], hT[:, c, :N], start=(c == 0), stop=(c == 7))
        k.copy(ACT, krf, ps[0:64, :N])
        rope_fm(krr, krf, N, t0)
        if not samp:
            k.copy(ACT, KTr[:, t0:t0 + N], krr)
        for ch in range(nch):
            psT = PS[6 + ch % 2]
            k.transpose(psT[0:nt, 0:64], krr[:, ch * 128:ch * 128 + nt], ident[0:64, 0:64])
            ko = a8[0:nt, 2 * NG + 256 + (ch % 2) * 64:2 * NG + 256 + (ch % 2) * 64 + 64]
            k.copy(ACT, ko, psT[0:nt, 0:64])
            if samp:
                k.dma(oq(), rope_s, ko)
            else:
                k.dma(oq(), rope_p[t0 + ch * 128:t0 + (ch + 1) * 128, :], ko)
        Wq = c_w_uq[0]
        qrf = a8[0:64, 0:N]
        for hb in range(2):
            wq = wload(Wq[:, hb * 768:(hb + 1) * 768].rearrange("(c p) n -> p c n", p=128), 4, 768, key=("c_uq", hb))
            for hl in range(4):
                h = hb * 4 + hl
                ps = pnext()
                for c in range(4):
                    k.mm(ps[:, :N], wq[:, c, hl * 192:hl * 192 + 128], cqn[:, c, :N], start=(c == 0), stop=(c == 3))
                k.copy(ACT, qnsb[:, :N], ps[:, :N])
                ps2 = pnext()
                for c in range(4):
                    k.mm(ps2[0:64, :N], wq[:, c, hl * 192 + 128:hl * 192 + 192], cqn[:, c, :N], start=(c == 0), stop=(c == 3))
                k.copy(ACT, qrf, ps2[0:64, :N])
                rope_fm(QrT[0:64, h, :N], qrf, N, t0)
                for cc in range(2):
                    ps3 = pnext()
                    k.mm(ps3[:, :N], WukT[:, h, cc * 128:(cc + 1) * 128], qnsb[:, :N])
                    k.copy(ACT, QlT[cc][:, h, :N], ps3[:, :N])
        if not samp:
            blocks = [(ql, hg) for ql in range(4) for hg in range(2)]
            psos = [PS[3], PS[6], PS[7]]

            def qk(bi, kt):
                ql, hg = blocks[bi]
                hs_ = slice(hg * 4, (hg + 1) * 4)
                qc = slice(ql * 128, (ql + 1) * 128)
                psS = PS[4 + kt % 2]
                kc = slice(kt * 128, (kt + 1) * 128)
                k.mm(psS.v.rearrange("p (h q) -> p h q", h=4), KTl[:, 0, kc], QlT[0][:, hs_, qc], start=True, stop=False)
                k.mm(psS.v.rearrange("p (h q) -> p h q", h=4), KTl[:, 1, kc], QlT[1][:, hs_, qc], start=False, stop=False)
                k.mm(psS.v.rearrange("p (h q) -> p h q", h=4), KTr[:, kc], QrT[0:64, hs_, qc], start=False, stop=True)

            def upproj(bi):
                ql, hg = blocks[bi]
                qc = slice(ql * 128, (ql + 1) * 128)
                for hl in range(4):
                    h = hg * 4 + hl
                    pso = psos[st["po"] % 3]
                    st["po"] += 1
                    for cc in range(2):
                        k.mm(pso[:, 0:128], Wuv[:, cc, h, :], olat[:, cc, hl * 128:(hl + 1) * 128], start=(cc == 0), stop=(cc == 1))
                    k.copy(ACT, attnT[:, h, qc], pso[:, 0:128])

            st["po"] = 0
            pend = None
            qk(0, 0)
            for bi, (ql, hg) in enumerate(blocks):
                qt = gi * 4 + ql
                for kt in range(qt + 1):
                    psS = PS[4 + kt % 2]
                    if kt + 1 <= qt:
                        qk(bi, kt + 1)
                    pt = PTb[kt % 2]
                    k.act(pt.v, psS.v, AF.Exp, scale=SCALE)
                    if kt == qt:
                        k.tt(DVE, pt.v.rearrange("p (h q) -> p h q", h=4), pt.v.rearrange("p (h q) -> p h q", h=4),
                             V(tril.ap.unsqueeze(1).broadcast_to([128, 4, 128]), tril), ALU.mult)
                    first, lastk = (kt == 0), (kt == qt)
                    k.mm(PS[0].v, KVt[:, kt, 0:128], pt.v, start=first, stop=lastk)
                    k.mm(PS[1].v, KVt[:, kt, 128:256], pt.v, start=first, stop=lastk)
                    k.mm(PS[2].v, onesb.v, pt.v, start=first, stop=lastk)
                    if kt == 0 and pend is not None:
                        upproj(pend)
                        pend = None
                if bi + 1 < len(blocks):
                    qk(bi + 1, 0)
                k.op(DVE, lambda e: e.reciprocal(rsum.ap, PS[2].ap), reads=[PS[2]], writes=[rsum])
                for cc in range(2):
                    k.tt(DVE, olat[:, cc, :], PS[cc].v, rsum.v, ALU.mult)
                pend = bi
            upproj(pend)
        else:

            kvs = arena[9].v[0:16, 0:256]
            kvsT = arena[9].v[:, 256:256 + 48].rearrange("p (c n) -> p c n", c=3)
            for cc in range(2):
                psT = PS[6 + cc]
                k.transpose(psT.v.bitcast(BF16)[:, 0:16], kvs[:, cc * 128:(cc + 1) * 128], identb[0:16, 0:16])
                k.copy(ACT, kvsT[:, cc, :], psT.v.bitcast(BF16)[:, 0:16])
            k.copy(ACT, kvsT[0:64, 2, :], krr)
            gb = [KVt.v.rearrange("p t c -> p (t c)"), KTl.v.rearrange("p t c -> p (t c)"), arena[0].v, arena[1].v, arena[2].v]
            gcnt = [0]
            olT = arena[9].v[:, 512:512 + 64].rearrange("p (c n) -> p c n", c=2)
            olb = arena[9].v[0:32, 1024:1024 + 256]
            for j in range(4):
                qs = slice(4 * j, 4 * j + 4)
                po, psum_ = PS[0], PS[1]
                ntile = 0
                def pv(ptile, kv_tile, nk, first, lastt):
                    k.mm(po[0:32, 0:256], ptile, kv_tile, start=first, stop=lastt)
                    k.mm(psum_[0:32, 0:1], ptile, onesb[0:nk, 0:1], start=first, stop=lastt)
                for blk2 in range(8):
                    psS = PS[4 + blk2 % 2]
                    pts = PTs[blk2 % 2]
                    gpair = []
                    for bb in range(2):
                        blk = blk2 * 2 + bb
                        g = gb[gcnt[0] % len(gb)]
                        gcnt[0] += 1
                        k.dma_raw(POOL, lambda e, g=g, j=j, blk=blk: e.indirect_dma_start(
                            out=g.ap[:, 0:2048], out_offset=None, in_=cache_lat,
                            in_offset=bass.IndirectOffsetOnAxis(ap=idxi.ap[:, j, blk:blk + 1], axis=0),
                            bounds_check=bcreg(e), oob_is_err=False), reads=[idxi], writes=[g])
                        k.dma_raw(POOL, lambda e, g=g, j=j, blk=blk: e.indirect_dma_start(
                            out=g.ap[:, 2048:2560], out_offset=None, in_=cache_rope,
                            in_offset=bass.IndirectOffsetOnAxis(ap=idxi.ap[:, j, blk:blk + 1], axis=0),
                            bounds_check=bcreg(e), oob_is_err=False), reads=[idxi], writes=[g])
                        gpair.append(g)
                    def tr_stage(ti):
                        g = gpair[ti // 8]
                        r = ti % 8
                        psT = PS[6 + ti % 2]
                        kts = KTs[ti % 2]
                        pb = psT.v.bitcast(BF16)
                        for cc in range(2):
                            k.transpose(pb[:, cc * 128:(cc + 1) * 128], g[:, r * 256 + cc * 128:r * 256 + (cc + 1) * 128], identb.v)
                        k.transpose(pb[0:64, 256:384], g[:, 2048 + r * 64:2048 + (r + 1) * 64], identb.v)
                        k.copy(ACT if ti % 2 == 0 else DVE, kts.rearrange("p c n -> p (c n)"), pb[:, 0:384])
                    def qk_stage(ti):
                        kts = KTs[ti % 2]
                        so = psS[:, ti * 32:(ti + 1) * 32].rearrange("p (h t) -> p h t", h=8)
                        k.mm(so, kts[:, 0, :], QlT[0][:, :, qs], start=True, stop=False)
                        k.mm(so, kts[:, 1, :], QlT[1][:, :, qs], start=False, stop=False)
                        k.mm(so, kts[0:64, 2, :], QrT[0:64, :, qs], start=False, stop=True)
                    tr_stage(0)
                    for ti in range(16):
                        if ti + 1 < 16:
                            tr_stage(ti + 1)
                        qk_stage(ti)
                    k.act(pts.v, psS.v, AF.Exp, scale=SCALE)
                    for bb in range(2):
                        g = gpair[bb]
                        for r in range(8):
                            ti = bb * 8 + r
                            pv(pts[:, ti * 32:(ti + 1) * 32], g[:, r * 256:(r + 1) * 256], 128, ntile == 0, False)
                            ntile += 1
                psN = PS[4]
                so = psN[0:16, 0:32].rearrange("p (h t) -> p h t", h=8)
                k.mm(so, kvsT[:, 0, :], QlT[0][:, :, qs], start=True, stop=False)
                k.mm(so, kvsT[:, 1, :], QlT[1][:, :, qs], start=False, stop=False)
                k.mm(so, kvsT[0:64, 2, :], QrT[0:64, :, qs], start=False, stop=True)
                ptn = PTs[0]
                k.act(ptn[0:16, 0:32], psN[0:16, 0:32], AF.Exp, scale=SCALE)
                k.tt(DVE, ptn[0:16, 0:32], ptn[0:16, 0:32], nmask[:, j, :], ALU.mult)
                pv(ptn[0:16, 0:32], kvs, 16, False, True)
                k.op(DVE, lambda e: e.reciprocal(dsm.ap[:, 0:1], psum_.ap[0:32, 0:1]), reads=[psum_], writes=[dsm])
                k.ts(DVE, olb, po[0:32, 0:256], dsm[:, 0:1], ALU.mult)
                for cc in range(2):
                    psT = PS[6 + cc]
                    k.transpose(psT.v.bitcast(BF16)[:, 0:32], olb[:, cc * 128:(cc + 1) * 128], identb[0:32, 0:32])
                    k.copy(ACT, olT[:, cc, :], psT.v.bitcast(BF16)[:, 0:32])
                for h in range(8):
                    pso = PS[3]
                    for cc in range(2):
                        k.mm(pso[:, 0:4], Wuv[:, cc, h, :], olT[:, cc, h * 4:(h + 1) * 4], start=(cc == 0), stop=(cc == 1))
                    k.copy(ACT, attnT[:, h, qs], pso[:, 0:4])
        for blk in range(2):
            wt = wblk(c_w_out[0], blk, key=("c_out",))
            for d4 in range(4):
                d = blk * 4 + d4
                ps = pnext()
                for c in range(8):
                    k.mm(ps[:, :N], wt[:, c, d4 * 128:(d4 + 1) * 128], attnT[:, c, :N], start=(c == 0), stop=(c == 7))
                residual(ps, d, l, 2, N, samp)

    if 2 in mixers and depth > 2:
        prep_C()
    for (gk, gi) in groups:
        samp = gk == "s"
        st["first"] = (gk == "p" and gi == 0)
        N = 16 if samp else NG
        nt = 16 if samp else 128
        nch = 1 if samp else 4
        for ch in range(nch):
            xi = xin[ch % 2]
            src = xsm if samp else xp[gi * NG + ch * 128: gi * NG + (ch + 1) * 128, :]
            k.dma(oq(), xi[0:nt, :], src)
            for half in range(2):
                ps = PS[6 + half]
                for c4 in range(4):
                    c = half * 4 + c4
                    k.transpose(ps[:, c4 * 128:c4 * 128 + nt], xi[0:nt, c * 128:(c + 1) * 128], ident[0:nt, 0:nt])
                k.op(ACT, lambda e, ps=ps, half=half, ch=ch, nt=nt: e.mul(
                    xT.ap[:, half * 4:half * 4 + 4, ch * 128:ch * 128 + nt],
                    ps.ap.rearrange("p (c t) -> p c t", c=4)[:, :, 0:nt], ALPHA), reads=[ps], writes=[xT])
        mark(f"g{gk}{gi} load")
        for l in range(depth):
            kind, j = l % 3, l // 3
            if samp:
                set_modS(l)
            ensure_mods(l + 1)
            ensure_fold(l)
            mark(f"g{gk}{gi} L{l} mixer")
            if kind in mixers:
                if kind == 0:
                    mixer_A(l, j, N, samp)
                elif kind == 1:
                    mixer_B(l, N, samp)
                else:
                    mixer_C(l, N, samp, gi)
            mark(f"g{gk}{gi} L{l} ln1")
            layernorm(l, 0, N, False, samp)
            mark(f"g{gk}{gi} L{l} ffn")
            ffn(l, N, samp)
            mark(f"g{gk}{gi} L{l} ln2")
            layernorm(l, 1, N, l == depth - 1, samp)
            if dbg:
                for ch in range(nch):
                    for half in range(2):
                        ps = PS[6 + half]
                        for c4 in range(4):
                            c = half * 4 + c4
                            k.transpose(ps[0:nt, c4 * 128:(c4 + 1) * 128], xT[:, c, ch * 128:ch * 128 + nt], ident.v)
                        xi = xin[half]
                        k.copy(DVE, xi[0:nt, half * 512:(half + 1) * 512], ps[0:nt, :])
                        r0 = (2048 if samp else gi * NG + ch * 128)
                        k.dma(oq(), dbg_o[l, r0:r0 + nt, half * 512:(half + 1) * 512], xi[0:nt, half * 512:(half + 1) * 512])
        if (not samp) and gi == 3 and 1 in mixers and depth > 1:
            k.dma(oq(), hs_p.rearrange("h k v -> k h v"), Sst.v)
        for ch in range(nch):
            for half in range(2):
                ps = PS[6 + half]
                for c4 in range(4):
                    c = half * 4 + c4
                    k.transpose(ps[0:nt, c4 * 128:(c4 + 1) * 128], xT[:, c, ch * 128:ch * 128 + nt], ident.v)
                xi = xin[half]
                k.copy(DVE, xi[0:nt, half * 512:(half + 1) * 512], ps[0:nt, :])
                if samp:
                    k.dma(oq(), y_s[:, half * 512:(half + 1) * 512], xi[0:nt, half * 512:(half + 1) * 512])
                else:
                    r0 = gi * NG + ch * 128
                    k.dma(oq(), y_p[r0:r0 + 128, half * 512:(half + 1) * 512], xi[0:nt, half * 512:(half + 1) * 512])
    mark("end")
    k.finish()
    return nc, k


def make_core_inputs(inp, core, depth=DEPTH):
    f = np.float32
    cT = np.concatenate([inp["c_prompt"][core:core + 1], inp["c_sample"][4 * core:4 * core + 4]], 0).T
    lnv = np.stack([inp["ln1_g"], inp["ln1_b"], inp["ln2_g"], inp["ln2_b"]], 0).reshape(4, DEPTH, 8, 128).transpose(3, 0, 1, 2)
    tril = np.triu(np.ones((128, 128), f))
    bm = np.zeros((16, 16), f)
    for b in range(4):
        bm[4 * b:4 * b + 4, 4 * b:4 * b + 4] = np.triu(np.ones((4, 4), f))
    half = 32
    inv = (np.float32(10000.0) ** (-np.arange(half, dtype=f) / f(half))).astype(f)
    pos = np.concatenate([np.arange(2048), np.tile(16384 + np.arange(4), 4)]).astype(f)
    ang = (pos[None, :] * inv[:, None]).astype(f)
    cosT = np.concatenate([np.cos(ang), np.cos(ang)], 0).astype(f)
    sinT = np.concatenate([np.sin(ang), np.sin(ang)], 0).astype(f)
    rotm = np.zeros((64, 64), f)
    for i in range(32):
        rotm[i + 32, i] = -1.0
        rotm[i, i + 32] = 1.0
    nmask = np.zeros((16, 4, 32), f)
    for j_ in range(4):
        for kk in range(4):
            for hh in range(8):
                for tt_ in range(4):
                    if kk <= tt_:
                        nmask[4 * j_ + kk, j_, hh * 4 + tt_] = 1.0
    m0p = np.ones((128, NG), f); m0p[:, ::64] = 0
    m0s = np.ones((128, 16), f); m0s[:, ::4] = 0
    return {
        "xp": np.ascontiguousarray(inp["x_prompt"][core]),
        "xs": np.ascontiguousarray(inp["x_sample"][4 * core:4 * core + 4].reshape(16, D)),
        "cT": np.ascontiguousarray(cT),
        "w_ada": inp["w_ada"][:depth],
        "b_adaT": np.ascontiguousarray(inp["b_ada"].reshape(DEPTH, 48, 128).transpose(2, 0, 1)),
        "lnv": np.ascontiguousarray(lnv),
        "ffn_w1": inp["ffn_w1"][:depth], "ffn_w2": inp["ffn_w2"][:depth],
        "a_w_in": inp["a_w_in"], "a_ln_g": inp["a_ln_g"], "a_ln_b": inp["a_ln_b"],
        "a_w_s": inp["a_w_s"], "a_b_s": np.ascontiguousarray(inp["a_b_s"].reshape(2, 1024)), "a_w_out": inp["a_w_out"],
        "ident": np.eye(128, dtype=f), "tril": tril, "bmask": bm,
        "b_w_in": inp["b_w_in"], "b_w_out": inp["b_w_out"],
        "b_lbT": np.ascontiguousarray(inp["b_lb"].reshape(4, 8, 128).transpose(2, 0, 1)),
        "state_in": np.ascontiguousarray(inp["state_hgrn"][0, 4 * core:4 * core + 4]),
        "m0p": m0p, "m0s": m0s,
        "c_w_in": inp["c_w_in"], "c_g_qT": np.ascontiguousarray(inp["c_g_q"].reshape(4, 128).T), "c_g_kv": inp["c_g_kv"],
        "c_w_uq": np.ascontiguousarray(inp["c_w_uq"].reshape(1, 512, 1536)), "c_w_uk": inp["c_w_uk"], "c_w_uv": inp["c_w_uv"],
        "c_w_out": inp["c_w_out"], "cosT": cosT, "sinT": sinT, "rotm": rotm,
        "cache_lat": inp["cache_kv_latent"].reshape(5120 * 16, 2048), "cache_rope": inp["cache_k_rope"].reshape(5120 * 16, 512),
        "ptT": np.ascontiguousarray(inp["page_table"][4 * core:4 * core + 4].T.astype(np.int32)),
        "blkf": np.tile(np.arange(16, dtype=f)[None, :], (128, 1)), "nmask": nmask,
    }


IMPLEMENTED_MIXERS = (0, 1, 2)


def kernel(**inputs):
    inp = {k_: np.asarray(v) for k_, v in inputs.items()}
    nc, kb = build_program(mixers=IMPLEMENTED_MIXERS, depth=DEPTH, dbg=False)
    in_maps = [make_core_inputs(inp, c) for c in range(8)]
    res = run_bass_kernel_spmd(nc, in_maps, core_ids=list(range(8)))
    r = res.results
    f = np.float32
    y_prompt = np.stack([r[c]["y_p"] for c in range(8)], 0).astype(f)
    y_sample = np.concatenate([r[c]["y_s"].reshape(4, 4, D) for c in range(8)], 0).astype(f)
    hs_p = np.stack([r[c]["hs_p"] for c in range(8)], 0)[None].astype(f)
    hs_s = np.concatenate([r[c]["hs_s"] for c in range(8)], 0)[None].astype(f)
    lat_p = np.stack([r[c]["lat_p"] for c in range(8)], 0)[None].astype(f)
    rope_p = np.stack([r[c]["rope_p"] for c in range(8)], 0)[None].astype(f)
    lat_s = np.concatenate([r[c]["lat_s"].reshape(4, 4, 256) for c in range(8)], 0)[None].astype(f)
    rope_s = np.concatenate([r[c]["rope_s"].reshape(4, 4, 64) for c in range(8)], 0)[None].astype(f)
    v_s = np.concatenate([r[c]["v_s"].reshape(2, 4, 4, D) for c in range(8)], 1).astype(f)
    return (y_prompt, y_sample, hs_p, hs_s, lat_p, rope_p, lat_s, rope_s, v_s)
```

```python
import numpy as np
from contextlib import ExitStack
import concourse.bass as bass
import concourse.mybir as mybir

F32 = mybir.dt.float32
BF16 = mybir.dt.bfloat16
I32 = mybir.dt.int32
U32 = mybir.dt.uint32
AF = mybir.ActivationFunctionType
ALU = mybir.AluOpType
AX = mybir.AxisListType

PE, ACT, DVE, POOL, SP = "tensor", "scalar", "vector", "gpsimd", "sync"
COMPUTE = (PE, ACT, DVE, POOL)


class Buf:
    __slots__ = ("ap", "name", "lw", "rd", "dsem", "dcnt")

    def __init__(self, ap, name):
        self.ap = ap
        self.name = name
        self.lw = []
        self.rd = []
        self.dsem = None
        self.dcnt = 0

    def __getitem__(self, key):
        return V(self.ap[key], self)

    @property
    def v(self):
        return V(self.ap, self)


class V:
    __slots__ = ("ap", "buf")

    def __init__(self, ap, buf):
        self.ap = ap
        self.buf = buf

    def __getitem__(self, key):
        return V(self.ap[key], self.buf)

    def bitcast(self, dt):
        return V(self.ap.bitcast(dt), self.buf)

    def rearrange(self, s, **kw):
        return V(self.ap.rearrange(s, **kw), self.buf)

    def bc(self, shape):
        return V(self.ap.broadcast_to(shape), self.buf)


class Op:
    __slots__ = ("eng", "fn", "deps", "is_dma", "sem", "val", "milestone", "id")


class KB:
    def __init__(self, nc):
        self.nc = nc
        self.es = ExitStack()
        self.ops = []
        self.nsem = 0
        self.sems = []
        self.dma_sems = []
        self.n_alloc = 0

    def dram_in(self, name, shape, dtype=F32):
        return self.nc.dram_tensor(name, list(shape), dtype, kind="ExternalInput").ap()

    def dram_out(self, name, shape, dtype=F32):
        return self.nc.dram_tensor(name, list(shape), dtype, kind="ExternalOutput").ap()

    def sb(self, name, shape, dtype=F32):
        t = self.es.enter_context(self.nc.sbuf_tensor(name, list(shape), dtype))
        return Buf(t[:] if len(shape) == 1 else t[tuple(slice(None) for _ in shape)], name)

    def ps(self, name, shape, dtype=F32):
        t = self.es.enter_context(self.nc.psum_tensor(name, list(shape), dtype))
        return Buf(t[tuple(slice(None) for _ in shape)], name)

    def new_sem(self, name):
        s = self.es.enter_context(self.nc.semaphore(name))
        return s

    def _deps(self, eng, reads, writes, is_dma, dsem_buf):
        deps = set()
        for b in reads:
            deps.update(b.lw)
        for b in writes:
            deps.update(b.lw)
            deps.update(b.rd)
        return deps

    def _dma_deps(self, rb, wb, dsem):
        deps = set()
        for b in rb:
            deps.update(b.lw)
        for b in wb:
            deps.update(b.rd)
            for d in b.lw:
                if not (self.ops[d].is_dma and self.ops[d].sem == dsem):
                    deps.add(d)
        return deps

    def op(self, eng, fn, reads=(), writes=()):
        rb = []
        for r in reads:
            if r is None:
                continue
            b = r.buf if isinstance(r, V) else r
            if b not in rb:
                rb.append(b)
        wb = []
        for w in writes:
            if w is None:
                continue
            b = w.buf if isinstance(w, V) else w
            if b not in wb:
                wb.append(b)
        o = Op()
        o.eng = eng
        o.fn = fn
        o.is_dma = False
        o.deps = self._deps(eng, rb, wb, False, None)
        o.sem = None
        o.val = None
        o.milestone = False
        o.id = len(self.ops)
        self.ops.append(o)
        for b in wb:
            b.lw = [o.id]
            b.rd = []
        for b in rb:
            if b not in wb:
                b.rd = [d for d in b.rd if self.ops[d].is_dma or self.ops[d].eng != eng]
                b.rd.append(o.id)
        return o

    def dma(self, queue, out, in_, sem_buf=None, **kw):
        rb, wb = [], []
        if isinstance(in_, V):
            rb.append(in_.buf)
            in_ap = in_.ap
        else:
            in_ap = in_
        if isinstance(out, V):
            wb.append(out.buf)
            out_ap = out.ap
        else:
            out_ap = out
        if sem_buf is None:
            sem_buf = wb[0] if wb else rb[0]
        if sem_buf.dsem is None:
            sem_buf.dsem = len(self.dma_sems)
            self.dma_sems.append(self.new_sem("d_" + sem_buf.name))
        o = Op()
        o.eng = queue
        o.is_dma = True
        o.deps = self._dma_deps(rb, wb, sem_buf.dsem)
        sem_buf.dcnt += 16
        o.sem = sem_buf.dsem
        o.val = sem_buf.dcnt
        o.milestone = True
        o.id = len(self.ops)
        o.fn = (lambda e, oa=out_ap, ia=in_ap, kw=kw: e.dma_start(out=oa, in_=ia, **kw))
        self.ops.append(o)
        for b in wb:
            if b.lw and all(self.ops[d].is_dma and self.ops[d].sem == o.sem for d in b.lw) and not b.rd:
                b.lw = b.lw + [o.id]
            else:
                b.lw = [o.id]
            b.rd = []
        for b in rb:
            b.rd.append(o.id)
        return o

    def dma_raw(self, queue, fn, reads=(), writes=(), sem_buf=None):
        rb = [(r.buf if isinstance(r, V) else r) for r in reads]
        wb = [(w.buf if isinstance(w, V) else w) for w in writes]
        if sem_buf is None:
            sem_buf = wb[0] if wb else rb[0]
        if sem_buf.dsem is None:
            sem_buf.dsem = len(self.dma_sems)
            self.dma_sems.append(self.new_sem("d_" + sem_buf.name))
        o = Op()
        o.eng = queue
        o.is_dma = True
        o.deps = self._dma_deps(rb, wb, sem_buf.dsem)
        sem_buf.dcnt += 16
        o.sem = sem_buf.dsem
        o.val = sem_buf.dcnt
        o.milestone = True
        o.id = len(self.ops)
        o.fn = fn
        self.ops.append(o)
        for b in wb:
            if b.lw and all(self.ops[d].is_dma and self.ops[d].sem == o.sem for d in b.lw) and not b.rd:
                b.lw = b.lw + [o.id]
            else:
                b.lw = [o.id]
            b.rd = []
        for b in rb:
            b.rd.append(o.id)
        return o

    def finish(self, final_wait_engine=SP):
        nc = self.nc
        ops = self.ops
        for o in ops:
            for d in o.deps:
                p = ops[d]
                if not p.is_dma:
                    if p.eng == PE and o.eng == PE and not o.is_dma:
                        continue
                    p.milestone = True
        eng_sem = {}
        for e in COMPUTE:
            eng_sem[e] = self.new_sem("e_" + e)
        cnt = {e: 0 for e in COMPUTE}
        for o in ops:
            if not o.is_dma and o.milestone:
                cnt[o.eng] += 1
                o.sem = ("E", o.eng)
                o.val = cnt[o.eng]
        streams = {e: [] for e in (PE, ACT, DVE, POOL, SP)}
        known = {e: {} for e in streams}
        nwait = 0
        for o in ops:
            need = {}
            for d in o.deps:
                p = ops[d]
                if (not p.is_dma) and p.eng == PE and o.eng == PE and not o.is_dma:
                    continue
                key = p.sem
                if need.get(key, 0) < p.val:
                    need[key] = p.val
            kn = known[o.eng]
            for key, val in need.items():
                if kn.get(key, 0) >= val:
                    continue
                kn[key] = val
                streams[o.eng].append(("w", key, val))
                nwait += 1
            streams[o.eng].append(("o", o))
        fin = []
        for o in ops:
            pass
        dma_final = {}
        for o in ops:
            if o.is_dma:
                dma_final[o.sem] = max(dma_final.get(o.sem, 0), o.val)
        for key, val in dma_final.items():
            if known[final_wait_engine].get(key, 0) < val:
                streams[final_wait_engine].append(("w", key, val))
        for e in COMPUTE:
            if cnt[e] > 0:
                streams[final_wait_engine].append(("w", ("E", e), cnt[e]))
        self.stats = dict(n_ops=len(ops), n_wait=nwait,
                          per_eng={e: sum(1 for s in streams[e] if s[0] == "o") for e in streams},
                          n_dma_sems=len(self.dma_sems))

        def semh(key):
            if isinstance(key, tuple):
                return eng_sem[key[1]]
            return self.dma_sems[key]

        def replay(e, engobj):
            for s in streams[e]:
                if s[0] == "w":
                    engobj.wait_ge(semh(s[1]), s[2])
                else:
                    o = s[1]
                    ins = o.fn(engobj)
                    if o.is_dma:
                        ins.then_inc(self.dma_sems[o.sem], 16)
                    elif o.milestone:
                        ins.then_inc(eng_sem[o.eng], 1)

        with nc.Block() as block:
            @block.tensor
            def _(e):
                replay(PE, e)

            @block.scalar
            def _(e):
                replay(ACT, e)

            @block.vector
            def _(e):
                replay(DVE, e)

            @block.gpsimd
            def _(e):
                replay(POOL, e)

            @block.sync
            def _(e):
                replay(SP, e)
        self.es.close()
        return nc

    def mm(self, out, lhsT, rhs, start=True, stop=True, **kw):
        return self.op(PE, lambda e: e.matmul(out.ap, lhsT.ap, rhs.ap, start=start, stop=stop, **kw),
                       reads=[lhsT, rhs], writes=[out])

    def transpose(self, out, in_, ident):
        return self.op(PE, lambda e: e.transpose(out.ap, in_.ap, ident.ap),
                       reads=[in_, ident], writes=[out])

    def act(self, out, in_, func, bias=None, scale=None, accum_out=None, eng=ACT):
        kw = {}
        rd = [in_]
        if bias is not None:
            kw["bias"] = bias.ap if isinstance(bias, V) else bias
            if isinstance(bias, V):
                rd.append(bias)
        if scale is not None:
            kw["scale"] = scale.ap if isinstance(scale, V) else scale
            if isinstance(scale, V):
                rd.append(scale)
        wr = [out]
        if accum_out is not None:
            kw["accum_out"] = accum_out.ap
            wr.append(accum_out)
        return self.op(ACT, lambda e: e.activation(out.ap, in_.ap, func, **kw), reads=rd, writes=wr)

    def tt(self, eng, out, in0, in1, op):
        return self.op(eng, lambda e: e.tensor_tensor(out.ap, in0.ap, in1.ap, op),
                       reads=[in0, in1], writes=[out])

    def ts(self, eng, out, in0, s1, op0, s2=None, op1=None, accum_out=None):
        rd = [in0]
        a1 = s1.ap if isinstance(s1, V) else s1
        if isinstance(s1, V):
            rd.append(s1)
        a2 = s2.ap if isinstance(s2, V) else s2
        if isinstance(s2, V):
            rd.append(s2)
        wr = [out]
        kw = {}
        if accum_out is not None:
            kw["accum_out"] = accum_out.ap
            wr.append(accum_out)
        if op1 is None:
            return self.op(eng, lambda e: e.tensor_scalar(out.ap, in0.ap, a1, None, op0, **kw),
                           reads=rd, writes=wr)
        return self.op(eng, lambda e: e.tensor_scalar(out.ap, in0.ap, a1, a2, op0, op1, **kw),
                       reads=rd, writes=wr)

    def stt(self, eng, out, in0, scalar, in1, op0, op1):
        rd = [in0, in1]
        a = scalar.ap if isinstance(scalar, V) else scalar
        if isinstance(scalar, V):
            rd.append(scalar)
        return self.op(eng, lambda e: e.scalar_tensor_tensor(out.ap, in0.ap, a, in1.ap, op0, op1),
                       reads=rd, writes=[out])

    def copy(self, eng, out, in_):
        if eng == ACT:
            return self.op(ACT, lambda e: e.copy(out.ap, in_.ap), reads=[in_], writes=[out])
        return self.op(eng, lambda e: e.tensor_copy(out.ap, in_.ap), reads=[in_], writes=[out])

    def memset(self, eng, out, val):
        return self.op(eng, lambda e: e.memset(out.ap, val), reads=[], writes=[out])


from concourse.bass_utils import run_bass_kernel_spmd

D = 1024
DEPTH = 4
ALPHA = (2.0 * DEPTH) ** 0.25
EPS = 1e-6
NG = 512
NW = 3
SCALE = (128 + 64) ** -0.5


def build_program(mixers=(0, 1, 2), depth=DEPTH, dbg=False):
    nc = bass.Bass("TRN2", target_bir_lowering=False)
    k = KB(nc)
    xp = k.dram_in("xp", [2048, D])
    xsm = k.dram_in("xs", [16, D])
    cT = k.dram_in("cT", [D, 5])
    w_ada = k.dram_in("w_ada", [depth, D, 6 * D])
    b_adaT = k.dram_in("b_adaT", [128, DEPTH, 48])
    lnv = k.dram_in("lnv", [128, 4, DEPTH, 8])
    ffn_w1 = k.dram_in("ffn_w1", [depth, D, 4 * D])
    ffn_w2 = k.dram_in("ffn_w2", [depth, 4 * D, D])
    a_w_in = k.dram_in("a_w_in", [2, D, 2 * D])
    a_ln_g = k.dram_in("a_ln_g", [2, D])
    a_ln_b = k.dram_in("a_ln_b", [2, D])
    a_w_s = k.dram_in("a_w_s", [2, 8, 128, 128])
    a_b_s = k.dram_in("a_b_s", [2, 8 * 128])
    a_w_out = k.dram_in("a_w_out", [2, D, D])
    b_w_in = k.dram_in("b_w_in", [1, D, 4 * D])
    b_w_out = k.dram_in("b_w_out", [1, D, D])
    b_lbT = k.dram_in("b_lbT", [128, 4, 8])
    state_in = k.dram_in("state_in", [4, 8, 128, 128])
    m0pd = k.dram_in("m0p", [128, NG])
    m0sd = k.dram_in("m0s", [128, 16])
    c_w_in = k.dram_in("c_w_in", [1, D, 832])
    c_g_qT = k.dram_in("c_g_qT", [128, 4])
    c_g_kv = k.dram_in("c_g_kv", [1, 256])
    c_w_uq = k.dram_in("c_w_uq", [1, 512, 1536])
    c_w_uk = k.dram_in("c_w_uk", [1, 256, 8, 128])
    c_w_uv = k.dram_in("c_w_uv", [1, 256, 8, 128])
    c_w_out = k.dram_in("c_w_out", [1, D, D])
    cache_lat = k.dram_in("cache_lat", [5120 * 16, 2048])
    cache_rope = k.dram_in("cache_rope", [5120 * 16, 512])
    ptT = k.dram_in("ptT", [128, 4], I32)
    blkd = k.dram_in("blkf", [128, 16])
    nmaskd = k.dram_in("nmask", [16, 4, 32])
    cosd = k.dram_in("cosT", [64, 2064])
    sind = k.dram_in("sinT", [64, 2064])
    rmd = k.dram_in("rotm", [64, 64])
    identd = k.dram_in("ident", [128, 128])
    trild = k.dram_in("tril", [128, 128])
    bmaskd = k.dram_in("bmask", [16, 16])
    y_p = k.dram_out("y_p", [2048, D])
    y_s = k.dram_out("y_s", [16, D])
    v_s = k.dram_out("v_s", [2, 16, D])
    hs_p = k.dram_out("hs_p", [8, 128, 128])
    hs_s = k.dram_out("hs_s", [4, 8, 128, 128])
    lat_p = k.dram_out("lat_p", [2048, 256])
    rope_p = k.dram_out("rope_p", [2048, 64])
    lat_s = k.dram_out("lat_s", [16, 256])
    rope_s = k.dram_out("rope_s", [16, 64])
    dbg_o = k.dram_out("dbg_x", [depth, 2064, D]) if dbg else None

    ident = k.sb("identf", [128, 128])
    onesf = k.sb("onesf", [128, 128])
    tril = k.sb("trilf", [128, 128])
    bmask = k.sb("bmaskf", [16, 16])
    epsc = k.sb("epsc", [128, 1])
    scT = k.sb("scT", [128, 8, 5], BF16)
    scf = k.sb("scf", [128, 8, 5])
    badaT = k.sb("badaT", [128, DEPTH, 48])
    lnc = k.sb("lnc", [128, 4, DEPTH, 8])
    lncA = k.sb("lncA", [128, 4, DEPTH, 8])
    modD = k.sb("modD", [128, DEPTH, 48, 5])
    modS = k.sb("modS", [128, 6, 8, 16])
    modG2 = k.sb("modG2", [128, DEPTH, 2, 8])
    modB2 = k.sb("modB2", [128, DEPTH, 2, 8])
    xT = k.sb("xT", [128, 8, NG])
    hT = k.sb("hT", [128, 8, NG], BF16)
    ring = [k.sb(f"wr{i}", [128, 4096], BF16) for i in range(NW)]
    arena = [k.sb(f"ar{i}", [128, 4096], BF16) for i in range(10)]
    identb = k.sb("identb", [128, 128], BF16)
    Sst = k.sb("Sst", [128, 8, 128])
    Sbf = k.sb("Sbf", [128, 8, 128], BF16)
    lbe = k.sb("lbe", [128, 4, 8])
    lbc = k.sb("lbc", [128, 8])
    omlc = k.sb("omlc", [128, 8])
    lbs = k.sb("lbs", [128, 8])
    decT = k.sb("decT", [128, 8, 8])
    m0p = k.sb("m0p_sb", [128, NG])
    m0s = k.sb("m0s_sb", [128, 16])
    Klsb = [k.sb(f"Klsb{i}", [64, 128], BF16) for i in range(2)]
    ATsb = [k.sb(f"ATsb{i}", [64, 64], BF16) for i in range(2)]
    bss = k.sb("bss", [64, 8])
    brs = k.sb("brs", [64, 8])
    KVt = k.sb("KVt", [128, 16, 256], BF16)
    KTl = k.sb("KTl", [128, 2, 2048], BF16)
    KTr = k.sb("KTr", [64, 2048], BF16)
    WukT = k.sb("WukT", [128, 8, 256], BF16)
    Wuv = k.sb("Wuv", [128, 2, 8, 128], BF16)
    gqc = k.sb("gqc", [128, 4])
    gkvb = k.sb("gkvb", [128, 256])
    rotm = k.sb("rotm_sb", [64, 64])
    onesb = k.sb("onesb", [128, 128], BF16)
    cosg = k.sb("cosg", [64, NG])
    sing = k.sb("sing", [64, NG])
    PTb = [k.sb(f"PTb{i}", [128, NG], BF16) for i in range(2)]
    olat = k.sb("olat", [128, 2, NG], BF16)
    rsum = k.sb("rsum", [128, NG])
    onb = V(olat.ap.rearrange("p c n -> p (c n)")[0:64, :].rearrange("p (h v) -> p h v", h=8), olat)
    css = k.sb("css", [128, 4])
    pti = k.sb("pti", [128, 4], I32)
    ptf = k.sb("ptf", [128, 4])
    blkf = k.sb("blkf_sb", [128, 16])
    idxf = k.sb("idxf", [128, 4, 16])
    idxi = k.sb("idxi", [128, 4, 16], I32)
    nmask = k.sb("nmask_sb", [16, 4, 32])
    PTs = PTb
    KTs = [V(olat.ap.rearrange("p c n -> p (c n)")[:, 0:384].rearrange("p (c n) -> p c n", c=3), olat),
           V(rsum.ap.bitcast(BF16)[:, 0:384].rearrange("p (c n) -> p c n", c=3), rsum)]
    dsm = k.sb("dsm", [32, 4])
    tmpA = [k.sb(f"tmpA{i}", [128, NG]) for i in range(2)]
    rl = [k.sb(f"rl{i}", [128, NG], BF16) for i in range(2)]
    xin0 = k.sb("xin0", [128, D])
    xin = [xin0, xin0]
    bnst = k.sb("bnst", [128, 2, 6])
    bnmv = k.sb("bnmv", [128, 4])
    wsT = k.sb("wsT", [128, 8, 128], BF16)
    wsTs = k.sb("wsTs", [16, 8, 16], BF16)
    bsb = k.sb("bsb", [128, 8, 128])
    bsbs = k.sb("bsbs", [128, 8, 16])
    PS = [k.ps(f"ps{i}", [128, 512]) for i in range(8)]
    st = {"ri": 0, "pi": 0}

    bcache = {}

    def bcreg(e):
        if "r" not in bcache:
            cm = e.register("bcreg")
            bcache["r"] = cm.__enter__()
            e.reg_mov(bcache["r"], 5120 * 16 - 1)
        return bcache["r"]

    def oq():
        return SP if st.get("first", True) else POOL

    def pnext():
        p = PS[st["pi"] % 4]
        st["pi"] += 1
        return p

    NSCR = 96
    scr = nc.dram_tensor("wscratch", [NSCR, 128, 4096], BF16).ap()
    scrbuf = Buf(scr, "wscratch")
    wcache = {}

    def wload(src, a, b, key=None):
        buf = ring[st["ri"] % NW]
        st["ri"] += 1
        flat = buf.v[:, 0:a * b]
        view = flat.rearrange("p (a b) -> p a b", a=a)
        if key is not None and key in wcache:
            t = wcache[key]
            k.dma(SP, flat, V(scr[t][:, 0:a * b], scrbuf), sem_buf=buf)
            return view
        k.dma(POOL, view, src)
        if key is not None and len(wcache) < NSCR:
            t = len(wcache)
            wcache[key] = t
            k.dma(SP, V(scr[t][:, 0:a * b], scrbuf), flat, sem_buf=buf)
        return view

    def wblk(w2d, blk, width=512, key=None):
        kc = w2d.shape[0] // 128
        return wload(w2d[:, blk * width:(blk + 1) * width].rearrange("(c p) n -> p c n", p=128), kc, width,
                     key=None if key is None else (key, blk))

    k.dma(SP, ident.v, identd)
    k.copy(DVE, identb.v, ident.v)
    k.dma(SP, m0p.v, m0pd)
    k.dma(SP, m0s.v, m0sd)
    k.dma(SP, lbe.v, b_lbT)
    k.act(lbe.v, lbe.v, AF.Exp)
    k.op(DVE, lambda e: e.tensor_reduce(lbs.ap, lbe.ap.rearrange("p l h -> p h l"), AX.X, ALU.add), reads=[lbe], writes=[lbs])
    k.op(DVE, lambda e: e.reciprocal(lbs.ap, lbs.ap), reads=[lbs], writes=[lbs])
    k.tt(DVE, lbc.v, lbe[:, 1, :], lbs.v, ALU.mult)
    k.ts(DVE, omlc.v, lbc.v, -1.0, ALU.mult, 1.0, ALU.add)
    k.memset(DVE, Sst.v, 0.0)
    k.memset(DVE, Sbf.v, 0.0)
    k.dma(SP, tril.v, trild)
    k.dma(SP, bmask.v, bmaskd)
    k.dma(SP, badaT.v, b_adaT)
    k.dma(SP, lnc.v, lnv)
    k.dma(SP, scf.v, cT.rearrange("(c p) s -> p c s", p=128))
    k.memset(DVE, onesf.v, 1.0)
    k.memset(DVE, onesb.v, 1.0)
    k.dma(SP, gqc.v, c_g_qT)
    k.dma(SP, gkvb.v, c_g_kv.broadcast_to([128, 256]))
    k.dma(SP, rotm.v, rmd)
    k.dma(SP, pti.v, ptT)
    k.dma(SP, blkf.v, blkd)
    k.dma(SP, nmask.v, nmaskd)
    k.copy(DVE, ptf.v, pti.v)
    k.ts(DVE, ptf.v, ptf.v, 16.0, ALU.mult)
    k.tt(DVE, idxf.v, V(ptf.ap.unsqueeze(2).broadcast_to([128, 4, 16]), ptf),
         V(blkf.ap.unsqueeze(1).broadcast_to([128, 4, 16]), blkf), ALU.add)
    k.copy(DVE, idxi.v, idxf.v)
    k.memset(DVE, epsc.v, EPS)
    k.act(scT.v, scf.v, AF.Silu)
    k.op(ACT, lambda e: e.mul(lncA.ap, lnc.ap, ALPHA), reads=[lnc], writes=[lncA])

    def ensure_mods(l):
        if l >= depth or l in st["mods"]:
            return
        st["mods"].add(l)
        for blk in range(12):
            wt = wblk(w_ada[l], blk)
            for f4 in range(4):
                j = blk * 4 + f4
                ps = pnext()
                for c in range(8):
                    k.mm(ps[:, 0:5], wt[:, c, f4 * 128:(f4 + 1) * 128], scT[:, c, :], start=(c == 0), stop=(c == 7))
                k.act(modD[:, l, j, :], ps[:, 0:5], AF.Identity, bias=badaT[:, l, j:j + 1])
        for kind in (1, 4):
            v = modD[:, l, kind * 8:(kind + 1) * 8, :]
            k.ts(DVE, v, v, 1.0, ALU.add, 1.0 / ALPHA, ALU.mult)

    def ensure_fold(l):
        if l >= depth or l in st["fold"]:
            return
        st["fold"].add(l)
        for which in range(2):
            if which == 0:
                ls, ks, ksh = l, 4, 3
            else:
                if l + 1 >= depth:
                    continue
                ls, ks, ksh = l + 1, 1, 0
            sc_ = modD[:, ls, ks * 8:(ks + 1) * 8, 0]
            sh_ = modD[:, ls, ksh * 8:(ksh + 1) * 8, 0]
            k.tt(DVE, modG2[:, l, which, :], lncA[:, 2 * which, l, :], sc_, ALU.mult)
            k.tt(DVE, modB2[:, l, which, :], lncA[:, 2 * which + 1, l, :], sc_, ALU.mult)
            k.tt(DVE, modB2[:, l, which, :], modB2[:, l, which, :], sh_, ALU.add)

    st["mods"] = set()
    st["fold"] = set()
    ensure_mods(0)
    ensure_mods(1)
    ensure_fold(0)

    groups = [("p", g) for g in range(4)] + [("s", 0)]
    marks = []
    k.marks = marks

    def mark(lbl):
        marks.append((lbl, sum(1 for o in k.ops if o.eng == PE)))

    def set_modS(l):
        for kind in range(6):
            src = modD[:, l, kind * 8:(kind + 1) * 8, 1:5]
            k.copy(DVE, modS[:, kind, :, :].rearrange("p c (s t) -> p c s t", t=4),
                   V(src.ap.unsqueeze(3).broadcast_to([128, 8, 4, 4]), src.buf))

    def modulate(l, ks, ksh, N, samp):
        if (not samp) and st.get("hready") == (l, ks):
            st["hready"] = None
            return
        if not samp:
            for c in range(8):
                k.act(hT[:, c, :N], xT[:, c, :N], AF.Identity, scale=modD[:, l, ks * 8 + c, 0:1],
                      bias=modD[:, l, ksh * 8 + c, 0:1])
        else:
            t = arena[6].v.bitcast(F32)[:, 0:128].rearrange("p (c n) -> p c n", c=8)
            k.tt(DVE, t, xT[:, :, :N], modS[:, ks, :, :], ALU.mult)
            k.tt(DVE, hT[:, :, :N], t, modS[:, ksh, :, :], ALU.add)

    ysq_ = arena[4].v.rearrange("p (c n) -> p c n", c=8)
    ybf_ = arena[5].v.rearrange("p (c n) -> p c n", c=8)
    st["lnq"] = []
    st["lnn"] = 0

    def ln_stats_mm(c, N):
        n = st["lnn"]
        k.mm(PS[4][:, :N], onesb.v, ybf_[:, c, :N], start=(n == 0), stop=(n == 7))
        k.mm(PS[5][:, :N], onesb.v, ysq_[:, c, :N], start=(n == 0), stop=(n == 7))
        st["lnn"] = n + 1

    def residual(ps, c, l, kg, N, samp):
        if not samp:
            q = st["lnq"]
            lag = 1 if kg == 5 else 3
            while len(q) >= lag:
                ln_stats_mm(q.pop(0), N)
            k.stt(DVE, xT[:, c, :N], ps[:, :N], modD[:, l, kg * 8 + c, 0:1], xT[:, c, :N], ALU.mult, ALU.add)
            k.act(ysq_[:, c, :N], xT[:, c, :N], AF.Square)
            k.copy(DVE, ybf_[:, c, :N], xT[:, c, :N])
            q.append(c)
        else:
            t = tmpA[c % 2]
            k.tt(DVE, t[:, :N], ps[:, :N], modS[:, kg, c, :], ALU.mult)
            k.tt(DVE, xT[:, c, :N], t[:, :N], xT[:, c, :N], ALU.add)

    tnb = [Buf(None, f"tn{c}") for c in range(8)]

    def layernorm(l, which, N, final, samp):
        ysq = arena[4].v.rearrange("p (c n) -> p c n", c=8)
        ybf = arena[5].v.rearrange("p (c n) -> p c n", c=8)
        tn = [V(arena[7 + c // 4].ap.bitcast(F32)[:, (c % 4) * NG:(c % 4) * NG + N], tnb[c]) for c in range(8)]
        abuf = [arena[7 + c // 4] for c in range(8)]
        ps_s, ps_q = PS[4], PS[5]
        if st["lnn"] + len(st["lnq"]) == 8:
            while st["lnq"]:
                ln_stats_mm(st["lnq"].pop(0), N)
        else:
            assert st["lnn"] == 0 and not st["lnq"]
            for c in range(8):
                k.act(ysq[:, c, :N], xT[:, c, :N], AF.Square)
                k.copy(DVE, ybf[:, c, :N], xT[:, c, :N])
            for c in range(8):
                k.mm(ps_s[:, :N], onesb.v, ybf[:, c, :N], start=(c == 0), stop=(c == 7))
            for c in range(8):
                k.mm(ps_q[:, :N], onesb.v, ysq[:, c, :N], start=(c == 0), stop=(c == 7))
        st["lnn"] = 0
        stt_ = V(arena[6].ap.bitcast(F32), arena[6])
        mean, msq, sd, rstd = (stt_[:, i * NG:i * NG + N] for i in range(4))
        k.ts(DVE, mean, ps_s[:, :N], 1.0 / D, ALU.mult)
        k.tt(DVE, msq, mean, mean, ALU.mult)
        k.stt(DVE, msq, ps_q[:, :N], 1.0 / D, msq, ALU.mult, ALU.subtract)
        k.act(sd, msq, AF.Sqrt, bias=epsc.v)
        gsrc = lnc if final else lncA
        fuse = (not samp) and (not final) and (which == 0 or l + 1 < depth)
        for c in range(8):
            t = tn[c]
            k.op(DVE, lambda e, t=t, c=c: e.tensor_tensor(t.ap, xT.ap[:, c, :N], mean.ap, ALU.subtract),
                 reads=[xT, mean, abuf[c]], writes=[t])
        k.op(DVE, lambda e: e.reciprocal(rstd.ap, sd.ap), reads=[sd], writes=[rstd])
        for c in range(8):
            t = tn[c]
            k.op(DVE, lambda e, t=t: e.tensor_tensor(t.ap, t.ap, rstd.ap, ALU.mult), reads=[t, rstd, abuf[c]], writes=[t])
        if fuse:
            for c in range(8):
                t = tn[c]
                k.op(ACT, lambda e, t=t, c=c: e.activation(hT.ap[:, c, :N], t.ap, AF.Identity,
                                                            scale=modG2.ap[:, l, which, c:c + 1],
                                                            bias=modB2.ap[:, l, which, c:c + 1]),
                     reads=[t, modG2, modB2, abuf[c]], writes=[hT])
        for c in range(8):
            t = tn[c]
            k.op(ACT, lambda e, t=t, c=c: e.activation(xT.ap[:, c, :N], t.ap, AF.Identity,
                                                        scale=gsrc.ap[:, 2 * which, l, c:c + 1],
                                                        bias=gsrc.ap[:, 2 * which + 1, l, c:c + 1]),
                 reads=[t, gsrc, abuf[c]], writes=[xT])
        st["hready"] = (l, 4) if (fuse and which == 0) else ((l + 1, 1) if fuse else None)

    def ffn(l, N, samp):
        modulate(l, 4, 3, N, samp)
        hid = [arena[i].v.rearrange("p (c n) -> p c n", c=8) for i in range(4)]
        for blk in range(8):
            wt = wblk(ffn_w1[l], blk, key=("w1", l))
            for f4 in range(4):
                f = blk * 4 + f4
                ps = pnext()
                for c in range(8):
                    k.mm(ps[:, :N], wt[:, c, f4 * 128:(f4 + 1) * 128], hT[:, c, :N], start=(c == 0), stop=(c == 7))
                r = rl[f % 2]
                k.act(r[:, :N], ps[:, :N], AF.Relu)
                k.tt(DVE, hid[f // 8][:, f % 8, :N], r[:, :N], r[:, :N], ALU.mult)
        for d in range(8):
            wt = wload(ffn_w2[l][:, d * 128:(d + 1) * 128].rearrange("(c p) n -> p c n", p=128), 32, 128, key=("w2", l, d))
            ps = pnext()
            for c in range(32):
                k.mm(ps[:, :N], wt[:, c, :], hid[c // 8][:, c % 8, :N], start=(c == 0), stop=(c == 31))
            residual(ps, d, l, 5, N, samp)

    def prep_A(j):
        wsf = arena[6].v.bitcast(F32)[:, 0:1024].rearrange("p (g s) -> p g s", g=8)
        k.dma(oq(), wsf, a_w_s[j].rearrange("g t s -> t g s"))
        for g in range(8):
            ps = PS[6 + g % 2]
            k.transpose(ps[:, 0:128], wsf[:, g, :], ident.v)
            k.tt(DVE, wsT[:, g, :], ps[:, 0:128], tril.v, ALU.mult)
        k.dma(oq(), bsb.v.rearrange("p g t -> p (g t)"), a_b_s[j:j + 1, :].broadcast_to([128, 1024]))
        k.memset(DVE, wsTs.v, 0.0)
        for b in range(4):
            k.dma(oq(), wsTs[4 * b:4 * b + 4, :, 4 * b:4 * b + 4], wsT[0:4, :, 0:4])
        k.copy(DVE, bsbs.v.rearrange("p g (s t) -> p g s t", t=4),
               V(bsb.ap[:, :, 0:4].unsqueeze(2).broadcast_to([128, 8, 4, 4]), bsb))

    def mixer_A(l, j, N, samp):
        prep_A(j)
        modulate(l, 1, 0, N, samp)
        nt = 16 if samp else 128
        nch = 1 if samp else N // 128
        uT = arena[0].v.rearrange("p (c n) -> p c n", c=8)
        oT = arena[1].v.rearrange("p (c n) -> p c n", c=8)
        vf = [V(arena[2].ap.bitcast(F32), arena[2]), V(arena[3].ap.bitcast(F32), arena[3])]
        vln = arena[4].v.rearrange("p (c n) -> p c n", c=4)
        vsout = V(arena[7].ap.bitcast(F32)[0:16, 0:1024], arena[7])
        lng = V(arena[5].ap.bitcast(F32)[:, 0:1024], arena[5])
        lnb = V(arena[5].ap.bitcast(F32)[:, 1024:2048], arena[5])
        k.dma(oq(), lng, a_ln_g[j:j + 1, :].broadcast_to([128, 1024]))
        k.dma(oq(), lnb, a_ln_b[j:j + 1, :].broadcast_to([128, 1024]))
        for blk in range(2):
            wt = wblk(a_w_in[j], blk, key=("a_in", j))
            for f4 in range(4):
                f = blk * 4 + f4
                ps = pnext()
                for c in range(8):
                    k.mm(ps[:, :N], wt[:, c, f4 * 128:(f4 + 1) * 128], hT[:, c, :N], start=(c == 0), stop=(c == 7))
                k.act(uT[:, f, :N], ps[:, :N], AF.Gelu_apprx_tanh)
        def vch(ch):
            return vf[ch // 2][:, (ch % 2) * 1024:(ch % 2) * 1024 + 1024]
        for blk in range(2, 4):
            wt = wblk(a_w_in[j], blk, key=("a_in", j))
            for ch in range(nch):
                ps = pnext()
                for c in range(8):
                    k.mm(ps[0:nt, :], hT[:, c, ch * 128:ch * 128 + nt], wt[:, c, :], start=(c == 0), stop=(c == 7))
                k.act(vch(ch)[0:nt, (blk - 2) * 512:(blk - 1) * 512], ps[0:nt, :], AF.Gelu_apprx_tanh)
        for ch in range(nch):
            v = vch(ch)
            for hh in range(2):
                k.op(DVE, lambda e, hh=hh, v=v: e.bn_stats(bnst.ap[0:nt, hh, :], v.ap[0:nt, hh * 512:(hh + 1) * 512]),
                     reads=[v], writes=[bnst])
            k.op(DVE, lambda e: e.bn_aggr(bnmv.ap[0:nt, 0:2], bnst.ap[0:nt].rearrange("p a b -> p (a b)")),
                 reads=[bnst], writes=[bnmv])
            k.act(bnmv[0:nt, 2:3], bnmv[0:nt, 1:2], AF.Sqrt, bias=epsc[0:nt, :])
            k.op(DVE, lambda e: e.reciprocal(bnmv.ap[0:nt, 3:4], bnmv.ap[0:nt, 2:3]), reads=[bnmv], writes=[bnmv])
            k.ts(DVE, v[0:nt, :], v[0:nt, :], bnmv[0:nt, 0:1], ALU.subtract, bnmv[0:nt, 3:4], ALU.mult)
            k.tt(DVE, v[0:nt, :], v[0:nt, :], lng[0:nt, :], ALU.mult)
            if samp:
                k.tt(DVE, vsout, v[0:nt, :], lnb[0:nt, :], ALU.add)
                k.dma(oq(), v_s[j], vsout)
                k.copy(DVE, vln[0:nt, ch, :], vsout)
            else:
                k.tt(DVE, vln[0:nt, ch, :], v[0:nt, :], lnb[0:nt, :], ALU.add)
        for g in range(8):
            ps = pnext()
            for ch in range(nch):
                if samp:
                    k.mm(ps[:, 0:16], vln[0:16, 0, g * 128:(g + 1) * 128], wsTs[:, g, :])
                else:
                    k.mm(ps[:, ch * 128:(ch + 1) * 128], vln[:, ch, g * 128:(g + 1) * 128], wsT[:, g, :])
            t = tmpA[g % 2]
            if samp:
                k.tt(DVE, t[:, :N], ps[:, :N], bsbs[:, g, :], ALU.add)
            else:
                k.tt(DVE, t[:, :N].rearrange("p (c t) -> p c t", t=128), ps[:, :N].rearrange("p (c t) -> p c t", t=128),
                     V(bsb.ap[:, g, :].unsqueeze(1).broadcast_to([128, nch, 128]), bsb), ALU.add)
            k.tt(DVE, oT[:, g, :N], t[:, :N], uT[:, g, :N], ALU.mult)
        for blk in range(2):
            wt = wblk(a_w_out[j], blk, key=("a_out", j))
            for d4 in range(4):
                d = blk * 4 + d4
                ps = pnext()
                for c in range(8):
                    k.mm(ps[:, :N], wt[:, c, d4 * 128:(d4 + 1) * 128], oT[:, c, :N], start=(c == 0), stop=(c == 7))
                residual(ps, d, l, 2, N, samp)


    def mixer_B(l, N, samp):
        modulate(l, 1, 0, N, samp)
        C = 4 if samp else 64
        nchk = N // C
        mid, last = (1, 3) if samp else (31, 63)
        m0 = m0s if samp else m0p
        def fm(i):
            return arena[i].v.rearrange("p (c n) -> p c n", c=8)
        qT, QbT, QmT, KmT, KlT, sgT = fm(0), fm(1), fm(2), fm(3), fm(4), fm(5)
        ogT = fm(0)
        vtm = [arena[6].v.rearrange("p (c n) -> p c n", c=4), arena[7].v.rearrange("p (c n) -> p c n", c=4)]
        def vch(ci):
            return vtm[ci // 4][0:C, ci % 4, :]
        t8 = V(arena[8].ap.bitcast(F32), arena[8])
        t9 = V(arena[9].ap.bitcast(F32), arena[9])
        tset = [[t8[:, i * NG:i * NG + N] for i in range(4)] + [tmpA[0][:, :N]],
                [t9[:, i * NG:i * NG + N] for i in range(4)] + [tmpA[1][:, :N]]]
        osq = xin0[0:64, :]
        W = b_w_in[0]
        for blk in range(8):
            wt = wblk(W, blk, key=("b_in",))
            if blk in (4, 5):
                for ci in range(nchk):
                    ps = pnext()
                    for c in range(8):
                        k.mm(ps[0:C, :], hT[:, c, ci * C:(ci + 1) * C], wt[:, c, :], start=(c == 0), stop=(c == 7))
                    k.copy(ACT, vch(ci)[:, (blk - 4) * 512:(blk - 3) * 512], ps[0:C, :])
                continue
            for f4 in range(4):
                f = blk * 4 + f4
                h = f % 8
                ps = pnext()
                for c in range(8):
                    k.mm(ps[:, :N], wt[:, c, f4 * 128:(f4 + 1) * 128], hT[:, c, :N], start=(c == 0), stop=(c == 7))
                if blk < 2:
                    k.act(qT[:, h, :N], ps[:, :N], AF.Silu)
                elif blk >= 6:
                    k.act(sgT[:, h, :N], ps[:, :N], AF.Silu)
                else:
                    tf, tk, tb, td, te = tset[h % 2]
                    k.act(tf, ps[:, :N], AF.Sigmoid)
                    k.ts(DVE, tf, tf, omlc[:, h:h + 1], ALU.mult, lbc[:, h:h + 1], ALU.add)
                    k.ts(DVE, tk, tf, -1.0, ALU.mult, 1.0, ALU.add)
                    k.act(tf, tf, AF.Ln)
                    k.op(DVE, lambda e, tb=tb, tlog=tf, m0=m0, N=N: e.tensor_tensor_scan(
                        tb.ap, m0.ap[:, :N], tlog.ap, 0.0, ALU.mult, ALU.add), reads=[m0, tf], writes=[tb])
                    b3 = tb.rearrange("p (a b) -> p a b", b=C)
                    d3 = td.rearrange("p (a b) -> p a b", b=C)
                    k.act(te, tb, AF.Exp)
                    k.tt(DVE, QbT[:, h, :N], qT[:, h, :N], te, ALU.mult)
                    k.tt(DVE, d3, b3, V(b3.ap[:, :, mid:mid + 1].broadcast_to([128, nchk, C]), b3.buf), ALU.subtract)
                    k.act(te, td, AF.Exp)
                    k.tt(DVE, QmT[:, h, :N], qT[:, h, :N], te, ALU.mult)
                    k.act(te, td, AF.Exp, scale=-1.0)
                    k.tt(DVE, KmT[:, h, :N], tk, te, ALU.mult)
                    k.tt(DVE, d3, V(b3.ap[:, :, last:last + 1].broadcast_to([128, nchk, C]), b3.buf), b3, ALU.subtract)
                    k.act(te, td, AF.Exp)
                    k.tt(DVE, KlT[:, h, :N], tk, te, ALU.mult)
                    k.act(decT[:, h, 0:nchk], b3[:, :, last], AF.Exp)
        for ci in range(nchk):
            if samp:
                k.dma(oq(), Sst.v, state_in[ci].rearrange("h k v -> k h v"))
                k.copy(ACT, Sbf.v, Sst.v)
            cs = slice(ci * C, (ci + 1) * C)
            def stage1(h):
                psT, psA = PS[6 + h % 2], (PS[4], PS[3])[h % 2]
                klsb, atsb = Klsb[h % 2], ATsb[h % 2]
                k.transpose(psT.v.bitcast(BF16)[0:C, 0:128], KlT[:, h, cs], identb.v)
                k.copy(ACT, klsb[0:C, :], psT.v.bitcast(BF16)[0:C, 0:128])
                k.mm(psA[0:C, 0:C], KmT[:, h, cs], QmT[:, h, cs])
                k.tt(DVE, atsb[0:C, 0:C], psA[0:C, 0:C], tril[0:C, 0:C], ALU.mult)
            def stage2(h):
                psS = PS[5]
                po = PS[h // 4]
                klsb, atsb = Klsb[h % 2], ATsb[h % 2]
                oc = slice((h % 4) * 128, (h % 4 + 1) * 128)
                k.mm(po[0:C, oc], atsb[0:C, 0:C], vch(ci)[:, h * 128:(h + 1) * 128], start=True, stop=False)
                k.mm(po[0:C, oc], QbT[:, h, cs], Sbf[:, h, :], start=False, stop=True)
                k.mm(psS[:, 0:128], klsb[0:C, :], vch(ci)[:, h * 128:(h + 1) * 128])
                k.stt(DVE, Sst[:, h, :], Sst[:, h, :], decT[:, h, ci:ci + 1], psS[:, 0:128], ALU.mult, ALU.add)
                k.copy(ACT, Sbf[:, h, :], Sst[:, h, :])
            stage1(0)
            for h in range(8):
                if h + 1 < 8:
                    stage1(h + 1)
                stage2(h)
            if samp:
                k.dma(oq(), hs_s[ci].rearrange("h k v -> k h v"), Sst.v)
            for hh in range(2):
                k.act(osq[0:C, hh * 512:(hh + 1) * 512], PS[hh][0:C, :], AF.Square)
            k.op(DVE, lambda e, osq=osq, C=C: e.tensor_reduce(bss.ap[0:C, :], osq.ap[0:C, :].rearrange("p (h v) -> p h v", h=8),
                                                               AX.X, ALU.add), reads=[osq], writes=[bss])
            k.act(brs[0:C, :], bss[0:C, :], AF.Sqrt, bias=epsc[0:C, :], scale=1.0 / 128)
            k.op(DVE, lambda e, C=C: e.reciprocal(brs.ap[0:C, :], brs.ap[0:C, :]), reads=[brs], writes=[brs])
            for hh in range(2):
                k.tt(DVE, onb[0:C, hh * 4:(hh + 1) * 4, :], PS[hh][0:C, :].rearrange("p (h v) -> p h v", h=4),
                     V(brs.ap[0:C, hh * 4:(hh + 1) * 4].unsqueeze(2).broadcast_to([C, 4, 128]), brs), ALU.mult)
            for h in range(8):
                psT = PS[6 + h % 2]
                k.transpose(psT.v.bitcast(BF16)[:, 0:C], onb[0:C, h, :], identb[0:C, 0:C])
                k.tt(DVE, ogT[:, h, cs], psT.v.bitcast(BF16)[:, 0:C], sgT[:, h, cs], ALU.mult)
        for blk in range(2):
            wt = wblk(b_w_out[0], blk, key=("b_out",))
            for d4 in range(4):
                d = blk * 4 + d4
                ps = pnext()
                for c in range(8):
                    k.mm(ps[:, :N], wt[:, c, d4 * 128:(d4 + 1) * 128], ogT[:, c, :N], start=(c == 0), stop=(c == 7))
                residual(ps, d, l, 2, N, samp)


    def prep_C():
        k.dma(POOL, Wuv.v, c_w_uv[0].rearrange("(cc p) h v -> p cc h v", p=128))
        wk = arena[9].v[:, 0:2048].rearrange("p (cc h d) -> p cc h d", cc=2, h=8)
        k.dma(POOL, wk, c_w_uk[0].rearrange("(cc p) h d -> p cc h d", p=128))
        for cc in range(2):
            for h in range(8):
                psT = PS[6 + h % 2]
                k.transpose(psT.v.bitcast(BF16)[:, 0:128], wk[:, cc, h, :], identb.v)
                k.copy(ACT, WukT[:, h, cc * 128:(cc + 1) * 128], psT.v.bitcast(BF16)[:, 0:128])

    def rope_fm(dst, src_f, N, c0):
        psr = PS[3]
        k.mm(psr[0:64, :N], rotm.v, src_f)
        t1 = V(arena[7].ap.bitcast(F32)[0:64, 0:N], arena[7])
        t2 = V(arena[7].ap.bitcast(F32)[0:64, NG:NG + N], arena[7])
        k.tt(DVE, t1, src_f, cosg[:, :N], ALU.mult)
        k.tt(DVE, t2, psr[0:64, :N], sing[:, :N], ALU.mult)
        k.tt(DVE, dst, t1, t2, ALU.add)

    def mixer_C(l, N, samp, gi):
        modulate(l, 1, 0, N, samp)
        nt = 16 if samp else 128
        nch = 1 if samp else N // 128
        t0 = 2048 if samp else gi * NG
        k.dma(oq(), cosg[:, :N], cosd[:, t0:t0 + N])
        k.dma(oq(), sing[:, :N], sind[:, t0:t0 + N])
        cqT = V(arena[0].ap.bitcast(F32), arena[0]).rearrange("p (c n) -> p c n", c=4)
        sqT = V(arena[1].ap.bitcast(F32), arena[1]).rearrange("p (c n) -> p c n", c=4)
        cqn = arena[2].v[:, 0:2048].rearrange("p (c n) -> p c n", c=4)
        qnsb = arena[2].v[:, 2048:2048 + NG]
        QlT = [arena[3].v.rearrange("p (h n) -> p h n", h=8), arena[4].v.rearrange("p (h n) -> p h n", h=8)]
        QrT = arena[5].v.rearrange("p (h n) -> p h n", h=8)
        attnT = arena[6].v.rearrange("p (h n) -> p h n", h=8)
        a8 = V(arena[8].ap.bitcast(F32), arena[8])
        krf = a8[0:64, 0:N]
        krr = a8[0:64, NG:NG + N]
        klf = a8[:, 2 * NG:2 * NG + 256]
        rstd = a8[:, 3 * NG:3 * NG + N]
        W = c_w_in[0]
        wt = wblk(W, 0, key=("c_in",))
        for f in range(4):
            ps = pnext()
            for c in range(8):
                k.mm(ps[:, :N], wt[:, c, f * 128:(f + 1) * 128], hT[:, c, :N], start=(c == 0), stop=(c == 7))
            k.copy(ACT, cqT[:, f, :N], ps[:, :N])
            k.act(sqT[:, f, :N], ps[:, :N], AF.Square)
        psq = PS[4]
        for f in range(4):
            k.mm(psq[:, :N], onesf.v, sqT[:, f, :N], start=(f == 0), stop=(f == 3))
        k.act(rstd, psq[:, :N], AF.Sqrt, bias=epsc.v, scale=1.0 / 512)
        k.op(DVE, lambda e, rstd=rstd: e.reciprocal(rstd.ap, rstd.ap), reads=[rstd], writes=[rstd])
        for f in range(4):
            k.stt(DVE, cqn[:, f, :N], cqT[:, f, :N], gqc[:, f:f + 1], rstd, ALU.mult, ALU.mult)
        wt = wload(W[:, 512:832].rearrange("(c p) n -> p c n", p=128), 8, 320, key=("c_in2",))
        for ch in range(nch):
            ps = pnext()
            for c in range(8):
                k.mm(ps[0:nt, 0:256], hT[:, c, ch * 128:ch * 128 + nt], wt[:, c, 0:256], start=(c == 0), stop=(c == 7))
            junk = sqT[0:nt, 0, 0:256]
            k.act(junk, ps[0:nt, 0:256], AF.Square, accum_out=css[0:nt, 0:1])
            k.act(css[0:nt, 1:2], css[0:nt, 0:1], AF.Sqrt, bias=epsc[0:nt, :], scale=1.0 / 256)
            k.op(DVE, lambda e, nt=nt: e.reciprocal(css.ap[0:nt, 2:3], css.ap[0:nt, 1:2]), reads=[css], writes=[css])
            k.stt(DVE, klf[0:nt, :], ps[0:nt, 0:256], css[0:nt, 2:3], gkvb[0:nt, :], ALU.mult, ALU.mult)
            if samp:
                k.dma(oq(), lat_s, klf[0:nt, :])
                kvs = arena[9].v[0:16, 0:256]
                k.copy(ACT, kvs, klf[0:nt, :])
            else:
                tile = gi * 4 + ch
                k.dma(oq(), lat_p[tile * 128:(tile + 1) * 128, :], klf[0:nt, :])
                k.copy(ACT, KVt[:, tile, :], klf[0:nt, :])
                for cc in range(2):
                    psT = PS[6 + cc]
                    k.transpose(psT.v.bitcast(BF16)[:, 0:128], KVt[:, tile, cc * 128:(cc + 1) * 128], identb.v)
                    k.copy(ACT, KTl[:, cc, tile * 128:(tile + 1) * 128], psT.v.bitcast(BF16)[:, 0:128])
        ps = pnext()
        for c in range(8):
            k.mm(ps[0:64, :N], wt[:, c, 256:320], hT[:, c, :N], start=(c == 0), stop=(c == 7))
        k.copy(ACT, krf, ps[0:64, :N])
        rope_fm(krr, krf, N, t0)
        if not samp:
            k.copy(ACT, KTr[:, t0:t0 + N], krr)
        for ch in range(nch):
            psT = PS[6 + ch % 2]
            k.transpose(psT[0:nt, 0:64], krr[:, ch * 128:ch * 128 + nt], ident[0:64, 0:64])
            ko = a8[0:nt, 2 * NG + 256 + (ch % 2) * 64:2 * NG + 256 + (ch % 2) * 64 + 64]
            k.copy(ACT, ko, psT[0:nt, 0:64])
            if samp:
                k.dma(oq(), rope_s, ko)
            else:
                k.dma(oq(), rope_p[t0 + ch * 128:t0 + (ch + 1) * 128, :], ko)
        Wq = c_w_uq[0]
        qrf = a8[0:64, 0:N]
        qnb = [qnsb, arena[2].v[:, 2048 + NG:2048 + 2 * NG]]

        def absorb(h):
            for cc in range(2):
                ps3 = pnext()
                k.mm(ps3[:, :N], WukT[:, h, cc * 128:(cc + 1) * 128], qnb[h % 2][:, :N])
                k.copy(ACT, QlT[cc][:, h, :N], ps3[:, :N])

        for hb in range(2):
            wq = wload(Wq[:, hb * 768:(hb + 1) * 768].rearrange("(c p) n -> p c n", p=128), 4, 768, key=("c_uq", hb))
            for hl in range(4):
                h = hb * 4 + hl
                ps = pnext()
                for c in range(4):
                    k.mm(ps[:, :N], wq[:, c, hl * 192:hl * 192 + 128], cqn[:, c, :N], start=(c == 0), stop=(c == 3))
                if h > 0:
                    absorb(h - 1)
                k.copy(ACT, qnb[h % 2][:, :N], ps[:, :N])
                ps2 = pnext()
                for c in range(4):
                    k.mm(ps2[0:64, :N], wq[:, c, hl * 192 + 128:hl * 192 + 192], cqn[:, c, :N], start=(c == 0), stop=(c == 3))
                k.copy(ACT, qrf, ps2[0:64, :N])
                rope_fm(QrT[0:64, h, :N], qrf, N, t0)
        absorb(7)
        if not samp:
            blocks = [(ql, hg) for ql in range(4) for hg in range(2)]
            psos = [PS[3], PS[6], PS[7]]

            def qk(bi, kt):
                ql, hg = blocks[bi]
                hs_ = slice(hg * 4, (hg + 1) * 4)
                qc = slice(ql * 128, (ql + 1) * 128)
                psS = PS[4 + kt % 2]
                kc = slice(kt * 128, (kt + 1) * 128)
                k.mm(psS.v.rearrange("p (h q) -> p h q", h=4), KTl[:, 0, kc], QlT[0][:, hs_, qc], start=True, stop=False)
                k.mm(psS.v.rearrange("p (h q) -> p h q", h=4), KTl[:, 1, kc], QlT[1][:, hs_, qc], start=False, stop=False)
                k.mm(psS.v.rearrange("p (h q) -> p h q", h=4), KTr[:, kc], QrT[0:64, hs_, qc], start=False, stop=True)

            def upproj(bi):
                ql, hg = blocks[bi]
                qc = slice(ql * 128, (ql + 1) * 128)
                for hl in range(4):
                    h = hg * 4 + hl
                    pso = psos[st["po"] % 3]
                    st["po"] += 1
                    for cc in range(2):
                        k.mm(pso[:, 0:128], Wuv[:, cc, h, :], olat[:, cc, hl * 128:(hl + 1) * 128], start=(cc == 0), stop=(cc == 1))
                    k.copy(ACT, attnT[:, h, qc], pso[:, 0:128])

            st["po"] = 0
            pend = None
            qk(0, 0)
            for bi, (ql, hg) in enumerate(blocks):
                qt = gi * 4 + ql
                for kt in range(qt + 1):
                    psS = PS[4 + kt % 2]
                    if kt + 1 <= qt:
                        qk(bi, kt + 1)
                    pt = PTb[kt % 2]
                    k.act(pt.v, psS.v, AF.Exp, scale=SCALE)
                    if kt == qt:
                        k.tt(DVE, pt.v.rearrange("p (h q) -> p h q", h=4), pt.v.rearrange("p (h q) -> p h q", h=4),
                             V(tril.ap.unsqueeze(1).broadcast_to([128, 4, 128]), tril), ALU.mult)
                    first, lastk = (kt == 0), (kt == qt)
                    k.mm(PS[0].v, KVt[:, kt, 0:128], pt.v, start=first, stop=lastk)
                    k.mm(PS[1].v, KVt[:, kt, 128:256], pt.v, start=first, stop=lastk)
                    k.mm(PS[2].v, onesb.v, pt.v, start=first, stop=lastk)
                    if kt == 0 and pend is not None:
                        upproj(pend)
                        pend = None
                if bi + 1 < len(blocks):
                    qk(bi + 1, 0)
                k.op(DVE, lambda e: e.reciprocal(rsum.ap, PS[2].ap), reads=[PS[2]], writes=[rsum])
                for cc in range(2):
                    k.tt(DVE, olat[:, cc, :], PS[cc].v, rsum.v, ALU.mult)
                pend = bi
            upproj(pend)
        else:

            kvs = arena[9].v[0:16, 0:256]
            kvsT = arena[9].v[:, 256:256 + 48].rearrange("p (c n) -> p c n", c=3)
            for cc in range(2):
                psT = PS[6 + cc]
                k.transpose(psT.v.bitcast(BF16)[:, 0:16], kvs[:, cc * 128:(cc + 1) * 128], identb[0:16, 0:16])
                k.copy(ACT, kvsT[:, cc, :], psT.v.bitcast(BF16)[:, 0:16])
            k.copy(ACT, kvsT[0:64, 2, :], krr)
            gb = [KVt.v.rearrange("p t c -> p (t c)"), KTl.v.rearrange("p t c -> p (t c)"), arena[0].v, arena[1].v, arena[2].v]
            gcnt = [0]
            olT = arena[9].v[:, 512:512 + 64].rearrange("p (c n) -> p c n", c=2)
            olb = arena[9].v[0:32, 1024:1024 + 256]
            for j in range(4):
                qs = slice(4 * j, 4 * j + 4)
                po, psum_ = PS[0], PS[1]
                ntile = 0
                def pv(ptile, kv_tile, nk, first, lastt):
                    k.mm(po[0:32, 0:256], ptile, kv_tile, start=first, stop=lastt)
                    k.mm(psum_[0:32, 0:1], ptile, onesb[0:nk, 0:1], start=first, stop=lastt)
                for blk2 in range(8):
                    psS = PS[4 + blk2 % 2]
                    pts = PTs[blk2 % 2]
                    gpair = []
                    for bb in range(2):
                        blk = blk2 * 2 + bb
                        g = gb[gcnt[0] % len(gb)]
                        gcnt[0] += 1
                        k.dma_raw(POOL, lambda e, g=g, j=j, blk=blk: e.indirect_dma_start(
                            out=g.ap[:, 0:2048], out_offset=None, in_=cache_lat,
                            in_offset=bass.IndirectOffsetOnAxis(ap=idxi.ap[:, j, blk:blk + 1], axis=0),
                            bounds_check=bcreg(e), oob_is_err=False), reads=[idxi], writes=[g])
                        k.dma_raw(POOL, lambda e, g=g, j=j, blk=blk: e.indirect_dma_start(
                            out=g.ap[:, 2048:2560], out_offset=None, in_=cache_rope,
                            in_offset=bass.IndirectOffsetOnAxis(ap=idxi.ap[:, j, blk:blk + 1], axis=0),
                            bounds_check=bcreg(e), oob_is_err=False), reads=[idxi], writes=[g])
                        gpair.append(g)
                    def tr_stage(ti):
                        g = gpair[ti // 8]
                        r = ti % 8
                        psT = PS[6 + ti % 2]
                        kts = KTs[ti % 2]
                        pb = psT.v.bitcast(BF16)
                        for cc in range(2):
                            k.transpose(pb[:, cc * 128:(cc + 1) * 128], g[:, r * 256 + cc * 128:r * 256 + (cc + 1) * 128], identb.v)
                        k.transpose(pb[0:64, 256:384], g[:, 2048 + r * 64:2048 + (r + 1) * 64], identb.v)
                        k.copy(ACT if ti % 2 == 0 else DVE, kts.rearrange("p c n -> p (c n)"), pb[:, 0:384])
                    def qk_stage(ti):
                        kts = KTs[ti % 2]
                        so = psS[:, ti * 32:(ti + 1) * 32].rearrange("p (h t) -> p h t", h=8)
                        k.mm(so, kts[:, 0, :], QlT[0][:, :, qs], start=True, stop=False)
                        k.mm(so, kts[:, 1, :], QlT[1][:, :, qs], start=False, stop=False)
                        k.mm(so, kts[0:64, 2, :], QrT[0:64, :, qs], start=False, stop=True)
                    tr_stage(0)
                    for ti in range(16):
                        if ti + 1 < 16:
                            tr_stage(ti + 1)
                        qk_stage(ti)
                    k.act(pts.v, psS.v, AF.Exp, scale=SCALE)
                    for bb in range(2):
                        g = gpair[bb]
                        for r in range(8):
                            ti = bb * 8 + r
                            pv(pts[:, ti * 32:(ti + 1) * 32], g[:, r * 256:(r + 1) * 256], 128, ntile == 0, False)
                            ntile += 1
                psN = PS[4]
                so = psN[0:16, 0:32].rearrange("p (h t) -> p h t", h=8)
                k.mm(so, kvsT[:, 0, :], QlT[0][:, :, qs], start=True, stop=False)
                k.mm(so, kvsT[:, 1, :], QlT[1][:, :, qs], start=False, stop=False)
                k.mm(so, kvsT[0:64, 2, :], QrT[0:64, :, qs], start=False, stop=True)
                ptn = PTs[0]
                k.act(ptn[0:16, 0:32], psN[0:16, 0:32], AF.Exp, scale=SCALE)
                k.tt(DVE, ptn[0:16, 0:32], ptn[0:16, 0:32], nmask[:, j, :], ALU.mult)
                pv(ptn[0:16, 0:32], kvs, 16, False, True)
                k.op(DVE, lambda e: e.reciprocal(dsm.ap[:, 0:1], psum_.ap[0:32, 0:1]), reads=[psum_], writes=[dsm])
                k.ts(DVE, olb, po[0:32, 0:256], dsm[:, 0:1], ALU.mult)
                for cc in range(2):
                    psT = PS[6 + cc]
                    k.transpose(psT.v.bitcast(BF16)[:, 0:32], olb[:, cc * 128:(cc + 1) * 128], identb[0:32, 0:32])
                    k.copy(ACT, olT[:, cc, :], psT.v.bitcast(BF16)[:, 0:32])
                for h in range(8):
                    pso = PS[3]
                    for cc in range(2):
                        k.mm(pso[:, 0:4], Wuv[:, cc, h, :], olT[:, cc, h * 4:(h + 1) * 4], start=(cc == 0), stop=(cc == 1))
                    k.copy(ACT, attnT[:, h, qs], pso[:, 0:4])
        for blk in range(2):
            wt = wblk(c_w_out[0], blk, key=("c_out",))
            for d4 in range(4):
                d = blk * 4 + d4
                ps = pnext()
                for c in range(8):
                    k.mm(ps[:, :N], wt[:, c, d4 * 128:(d4 + 1) * 128], attnT[:, c, :N], start=(c == 0), stop=(c == 7))
                residual(ps, d, l, 2, N, samp)

    if 2 in mixers and depth > 2:
        prep_C()
    for (gk, gi) in groups:
        samp = gk == "s"
        st["first"] = (gk == "p" and gi == 0)
        N = 16 if samp else NG
        nt = 16 if samp else 128
        nch = 1 if samp else 4
        for ch in range(nch):
            xi = xin[ch % 2]
            src = xsm if samp else xp[gi * NG + ch * 128: gi * NG + (ch + 1) * 128, :]
            k.dma(oq(), xi[0:nt, :], src)
            for half in range(2):
                ps = PS[6 + half]
                for c4 in range(4):
                    c = half * 4 + c4
                    k.transpose(ps[:, c4 * 128:c4 * 128 + nt], xi[0:nt, c * 128:(c + 1) * 128], ident[0:nt, 0:nt])
                k.op(ACT, lambda e, ps=ps, half=half, ch=ch, nt=nt: e.mul(
                    xT.ap[:, half * 4:half * 4 + 4, ch * 128:ch * 128 + nt],
                    ps.ap.rearrange("p (c t) -> p c t", c=4)[:, :, 0:nt], ALPHA), reads=[ps], writes=[xT])
        mark(f"g{gk}{gi} load")
        for l in range(depth):
            kind, j = l % 3, l // 3
            if samp:
                set_modS(l)
            ensure_mods(l + 1)
            ensure_fold(l)
            mark(f"g{gk}{gi} L{l} mixer")
            if kind in mixers:
                if kind == 0:
                    mixer_A(l, j, N, samp)
                elif kind == 1:
                    mixer_B(l, N, samp)
                else:
                    mixer_C(l, N, samp, gi)
            mark(f"g{gk}{gi} L{l} ln1")
            layernorm(l, 0, N, False, samp)
            mark(f"g{gk}{gi} L{l} ffn")
            ffn(l, N, samp)
            mark(f"g{gk}{gi} L{l} ln2")
            layernorm(l, 1, N, l == depth - 1, samp)
            if dbg:
                for ch in range(nch):
                    for half in range(2):
                        ps = PS[6 + half]
                        for c4 in range(4):
                            c = half * 4 + c4
                            k.transpose(ps[0:nt, c4 * 128:(c4 + 1) * 128], xT[:, c, ch * 128:ch * 128 + nt], ident.v)
                        xi = xin[half]
                        k.copy(DVE, xi[0:nt, half * 512:(half + 1) * 512], ps[0:nt, :])
                        r0 = (2048 if samp else gi * NG + ch * 128)
                        k.dma(oq(), dbg_o[l, r0:r0 + nt, half * 512:(half + 1) * 512], xi[0:nt, half * 512:(half + 1) * 512])
        if (not samp) and gi == 3 and 1 in mixers and depth > 1:
            k.dma(oq(), hs_p.rearrange("h k v -> k h v"), Sst.v)
        for ch in range(nch):
            for half in range(2):
                ps = PS[6 + half]
                for c4 in range(4):
                    c = half * 4 + c4
                    k.transpose(ps[0:nt, c4 * 128:(c4 + 1) * 128], xT[:, c, ch * 128:ch * 128 + nt], ident.v)
                xi = xin[half]
                k.copy(DVE, xi[0:nt, half * 512:(half + 1) * 512], ps[0:nt, :])
                if samp:
                    k.dma(oq(), y_s[:, half * 512:(half + 1) * 512], xi[0:nt, half * 512:(half + 1) * 512])
                else:
                    r0 = gi * NG + ch * 128
                    k.dma(oq(), y_p[r0:r0 + 128, half * 512:(half + 1) * 512], xi[0:nt, half * 512:(half + 1) * 512])
    mark("end")
    k.finish()
    return nc, k


def make_core_inputs(inp, core, depth=DEPTH):
    f = np.float32
    cT = np.concatenate([inp["c_prompt"][core:core + 1], inp["c_sample"][4 * core:4 * core + 4]], 0).T
    lnv = np.stack([inp["ln1_g"], inp["ln1_b"], inp["ln2_g"], inp["ln2_b"]], 0).reshape(4, DEPTH, 8, 128).transpose(3, 0, 1, 2)
    tril = np.triu(np.ones((128, 128), f))
    bm = np.zeros((16, 16), f)
    for b in range(4):
        bm[4 * b:4 * b + 4, 4 * b:4 * b + 4] = np.triu(np.ones((4, 4), f))
    half = 32
    inv = (np.float32(10000.0) ** (-np.arange(half, dtype=f) / f(half))).astype(f)
    pos = np.concatenate([np.arange(2048), np.tile(16384 + np.arange(4), 4)]).astype(f)
    ang = (pos[None, :] * inv[:, None]).astype(f)
    cosT = np.concatenate([np.cos(ang), np.cos(ang)], 0).astype(f)
    sinT = np.concatenate([np.sin(ang), np.sin(ang)], 0).astype(f)
    rotm = np.zeros((64, 64), f)
    for i in range(32):
        rotm[i + 32, i] = -1.0
        rotm[i, i + 32] = 1.0
    nmask = np.zeros((16, 4, 32), f)
    for j_ in range(4):
        for kk in range(4):
            for hh in range(8):
                for tt_ in range(4):
                    if kk <= tt_:
                        nmask[4 * j_ + kk, j_, hh * 4 + tt_] = 1.0
    m0p = np.ones((128, NG), f); m0p[:, ::64] = 0
    m0s = np.ones((128, 16), f); m0s[:, ::4] = 0
    return {
        "xp": np.ascontiguousarray(inp["x_prompt"][core]),
        "xs": np.ascontiguousarray(inp["x_sample"][4 * core:4 * core + 4].reshape(16, D)),
        "cT": np.ascontiguousarray(cT),
        "w_ada": inp["w_ada"][:depth],
        "b_adaT": np.ascontiguousarray(inp["b_ada"].reshape(DEPTH, 48, 128).transpose(2, 0, 1)),
        "lnv": np.ascontiguousarray(lnv),
        "ffn_w1": inp["ffn_w1"][:depth], "ffn_w2": inp["ffn_w2"][:depth],
        "a_w_in": inp["a_w_in"], "a_ln_g": inp["a_ln_g"], "a_ln_b": inp["a_ln_b"],
        "a_w_s": inp["a_w_s"], "a_b_s": np.ascontiguousarray(inp["a_b_s"].reshape(2, 1024)), "a_w_out": inp["a_w_out"],
        "ident": np.eye(128, dtype=f), "tril": tril, "bmask": bm,
        "b_w_in": inp["b_w_in"], "b_w_out": inp["b_w_out"],
        "b_lbT": np.ascontiguousarray(inp["b_lb"].reshape(4, 8, 128).transpose(2, 0, 1)),
        "state_in": np.ascontiguousarray(inp["state_hgrn"][0, 4 * core:4 * core + 4]),
        "m0p": m0p, "m0s": m0s,
        "c_w_in": inp["c_w_in"], "c_g_qT": np.ascontiguousarray(inp["c_g_q"].reshape(4, 128).T), "c_g_kv": inp["c_g_kv"],
        "c_w_uq": np.ascontiguousarray(inp["c_w_uq"].reshape(1, 512, 1536)), "c_w_uk": inp["c_w_uk"], "c_w_uv": inp["c_w_uv"],
        "c_w_out": inp["c_w_out"], "cosT": cosT, "sinT": sinT, "rotm": rotm,
        "cache_lat": inp["cache_kv_latent"].reshape(5120 * 16, 2048), "cache_rope": inp["cache_k_rope"].reshape(5120 * 16, 512),
        "ptT": np.ascontiguousarray(inp["page_table"][4 * core:4 * core + 4].T.astype(np.int32)),
        "blkf": np.tile(np.arange(16, dtype=f)[None, :], (128, 1)), "nmask": nmask,
    }


IMPLEMENTED_MIXERS = (0, 1, 2)


def kernel(**inputs):
    inp = {k_: np.asarray(v) for k_, v in inputs.items()}
    nc, kb = build_program(mixers=IMPLEMENTED_MIXERS, depth=DEPTH, dbg=False)
    in_maps = [make_core_inputs(inp, c) for c in range(8)]
    res = run_bass_kernel_spmd(nc, in_maps, core_ids=list(range(8)))
    r = res.results
    f = np.float32
    y_prompt = np.stack([r[c]["y_p"] for c in range(8)], 0).astype(f)
    y_sample = np.concatenate([r[c]["y_s"].reshape(4, 4, D) for c in range(8)], 0).astype(f)
    hs_p = np.stack([r[c]["hs_p"] for c in range(8)], 0)[None].astype(f)
    hs_s = np.concatenate([r[c]["hs_s"] for c in range(8)], 0)[None].astype(f)
    lat_p = np.stack([r[c]["lat_p"] for c in range(8)], 0)[None].astype(f)
    rope_p = np.stack([r[c]["rope_p"] for c in range(8)], 0)[None].astype(f)
    lat_s = np.concatenate([r[c]["lat_s"].reshape(4, 4, 256) for c in range(8)], 0)[None].astype(f)
    rope_s = np.concatenate([r[c]["rope_s"].reshape(4, 4, 64) for c in range(8)], 0)[None].astype(f)
    v_s = np.concatenate([r[c]["v_s"].reshape(2, 4, 4, D) for c in range(8)], 1).astype(f)
    return (y_prompt, y_sample, hs_p, hs_s, lat_p, rope_p, lat_s, rope_s, v_s)
```
